# Optimizing a Trainium2 kernel written in Bass

```python
import math
import jax, jax.numpy as jnp
from jax import lax
import numpy as np

D_MODEL = 1024
BATCH = 16
SEQ = 2048
DEPTH = 1

HEAD_DIM = 64
V_HEAD_DIM = 2 * HEAD_DIM
N_HEADS = D_MODEL // V_HEAD_DIM
QK_WIDTH = N_HEADS * 2 * HEAD_DIM
ATTN_WIDTH = N_HEADS * V_HEAD_DIM
Q_BLOCK = 128
CONV_WIDTH = D_MODEL
CONV_K = 3
PEER_HEADS = 8
N_KEYS = 128
N_EXPERTS = N_KEYS * N_KEYS
PEER_TOPK = 16
D_KEY = 256
D_HALF = D_KEY // 2
TOKEN_CHUNK = 128
EPS = 1e-6
SPLIT_SIZES = (QK_WIDTH, QK_WIDTH, ATTN_WIDTH, CONV_WIDTH, CONV_WIDTH, CONV_WIDTH, D_MODEL, D_MODEL)
SPLIT_POINTS = tuple(int(v) for v in np.cumsum(SPLIT_SIZES)[:-1])
IN_COLS = int(sum(SPLIT_SIZES))

kernel_name = "hybrid_diffattn_shortconv_peer"


def rmsnorm(x, g):
    xf = x.astype(jnp.float32)
    y = xf * lax.rsqrt(jnp.mean(xf * xf, axis=-1, keepdims=True) + EPS)
    return (y * g.astype(jnp.float32)).astype(x.dtype)


def alibi_slopes(n):
    return jnp.asarray(np.array([2.0 ** (-8.0 * (i + 1) / n) for i in range(n)], dtype=np.float32))


def diff_attention(q, k, v, lam, slopes):
    S = q.shape[1]
    scale = HEAD_DIM ** -0.5
    outs = []
    for i in range(S // Q_BLOCK):
        q0 = i * Q_BLOCK
        kv_len = q0 + Q_BLOCK
        qb = q[:, q0:kv_len]
        kb = k[:, :kv_len]
        vb = v[:, :kv_len]
        s = jnp.einsum('bqhpd,bkhpd->bhpqk', qb, kb).astype(jnp.float32) * scale
        dist = (q0 + jnp.arange(Q_BLOCK))[:, None] - jnp.arange(kv_len)[None, :]
        bias = -slopes[:, None, None, None] * dist.astype(jnp.float32)[None, None]
        s = jnp.where(dist >= 0, s + bias[None], -jnp.inf)
        p = jax.nn.softmax(s, axis=-1)
        a = p[:, :, 0] - lam * p[:, :, 1]
        outs.append(jnp.einsum('bhqk,bkhe->bqhe', a.astype(v.dtype), vb))
    return jnp.concatenate(outs, axis=1)


def short_conv(gate_b, gate_c, xin, conv_w):
    y = gate_c * xin
    z = lax.conv_general_dilated(
        y, conv_w.astype(y.dtype)[:, None, :], window_strides=(1,),
        padding=[(CONV_K - 1, 0)], dimension_numbers=('NWC', 'WIO', 'NWC'),
        feature_group_count=CONV_WIDTH)
    return gate_b * z


def peer(h, w_query, sub_keys, u, v):
    B, S, D = h.shape
    hc_all = h.reshape((B * S) // TOKEN_CHUNK, TOKEN_CHUNK, D)

    def chunk(hc):
        q = (hc @ w_query).reshape(TOKEN_CHUNK, PEER_HEADS, 2, D_HALF)
        s = jnp.einsum('thpd,hpkd->thpk', q, sub_keys)
        sv, si = lax.top_k(s, PEER_TOPK)
        cand = (sv[:, :, 0, :, None] + sv[:, :, 1, None, :]).reshape(TOKEN_CHUNK, PEER_HEADS, PEER_TOPK * PEER_TOPK)
        cidx = (si[:, :, 0, :, None] * N_KEYS + si[:, :, 1, None, :]).reshape(TOKEN_CHUNK, PEER_HEADS, PEER_TOPK * PEER_TOPK)
        tv, tp = lax.top_k(cand, PEER_TOPK)
        eidx = jnp.take_along_axis(cidx, tp, axis=-1)
        g = jax.nn.softmax(tv.astype(jnp.float32), axis=-1)
        ue = jnp.take(u, eidx, axis=0)
        a = jnp.einsum('thkd,td->thk', ue, hc)
        w = (g * jax.nn.gelu(a.astype(jnp.float32), approximate=False)).astype(hc.dtype)
        ve = jnp.take(v, eidx, axis=0)
        return jnp.einsum('thk,thkd->td', w, ve)

    return lax.map(chunk, hc_all).reshape(B, S, D)


def setup_inputs(seed: int = 0) -> dict:
    key = jax.random.key(seed)
    ks = jax.random.split(key, 20)
    f32 = jnp.float32
    sd = D_MODEL ** -0.5
    nrm = lambda k, shape, s: jax.random.normal(k, shape, f32) * s
    return {
        'x': nrm(ks[0], (BATCH, SEQ, D_MODEL), 1.0),
        'norm_mix_g': 1.0 + nrm(ks[1], (DEPTH, D_MODEL), 0.02),
        'w_in': nrm(ks[2], (DEPTH, D_MODEL, IN_COLS), sd),
        'q_norm_g': 1.0 + nrm(ks[3], (DEPTH, HEAD_DIM), 0.02),
        'k_norm_g': 1.0 + nrm(ks[4], (DEPTH, HEAD_DIM), 0.02),
        'lambda_q1': nrm(ks[5], (DEPTH, HEAD_DIM), 0.1),
        'lambda_k1': nrm(ks[6], (DEPTH, HEAD_DIM), 0.1),
        'lambda_q2': nrm(ks[7], (DEPTH, HEAD_DIM), 0.1),
        'lambda_k2': nrm(ks[8], (DEPTH, HEAD_DIM), 0.1),
        'subln_g': 1.0 + nrm(ks[9], (DEPTH, V_HEAD_DIM), 0.02),
        'w_attn_proj': nrm(ks[10], (DEPTH, ATTN_WIDTH, D_MODEL), ATTN_WIDTH ** -0.5),
        'conv_w': nrm(ks[11], (DEPTH, CONV_K, CONV_WIDTH), CONV_K ** -0.5),
        'w_conv_proj': nrm(ks[12], (DEPTH, CONV_WIDTH, D_MODEL), CONV_WIDTH ** -0.5),
        'w_out': nrm(ks[13], (DEPTH, D_MODEL, D_MODEL), sd),
        'norm_ffn_g': 1.0 + nrm(ks[14], (DEPTH, D_MODEL), 0.02),
        'peer_w_query': nrm(ks[15], (DEPTH, D_MODEL, PEER_HEADS * D_KEY), sd),
        'peer_sub_keys': nrm(ks[16], (DEPTH, PEER_HEADS, 2, N_KEYS, D_HALF), D_HALF ** -0.5),
        'peer_u': nrm(ks[17], (DEPTH, N_EXPERTS, D_MODEL), sd),
        'peer_v': nrm(ks[18], (DEPTH, N_EXPERTS, D_MODEL), (PEER_HEADS * PEER_TOPK) ** -0.5),
    }


def reference(x, norm_mix_g, w_in, q_norm_g, k_norm_g, lambda_q1, lambda_k1, lambda_q2, lambda_k2,
              subln_g, w_attn_proj, conv_w, w_conv_proj, w_out, norm_ffn_g,
              peer_w_query, peer_sub_keys, peer_u, peer_v):
    B, S, _ = x.shape
    slopes = alibi_slopes(N_HEADS)
    for l in range(DEPTH):
        lam_init = 0.8 - 0.6 * math.exp(-0.3 * l)
        h = rmsnorm(x, norm_mix_g[l])
        proj = h @ w_in[l]
        q, k, v, cb, cc, cx, ga, gc = jnp.split(proj, SPLIT_POINTS, axis=-1)
        q = rmsnorm(q.reshape(B, S, N_HEADS, 2, HEAD_DIM), q_norm_g[l])
        k = rmsnorm(k.reshape(B, S, N_HEADS, 2, HEAD_DIM), k_norm_g[l])
        v = v.reshape(B, S, N_HEADS, V_HEAD_DIM)
        lam = (jnp.exp(jnp.sum(lambda_q1[l].astype(jnp.float32) * lambda_k1[l].astype(jnp.float32)))
               - jnp.exp(jnp.sum(lambda_q2[l].astype(jnp.float32) * lambda_k2[l].astype(jnp.float32)))
               + lam_init)
        attn = diff_attention(q, k, v, lam, slopes)
        attn = rmsnorm(attn, subln_g[l]) * (1.0 - lam_init)
        attn = attn.reshape(B, S, ATTN_WIDTH) @ w_attn_proj[l]
        conv = short_conv(cb, cc, cx, conv_w[l]) @ w_conv_proj[l]
        mixed = jax.nn.sigmoid(ga) * attn + jax.nn.sigmoid(gc) * conv
        x = x + mixed @ w_out[l]
        h2 = rmsnorm(x, norm_ffn_g[l])
        x = x + peer(h2, peer_w_query[l], peer_sub_keys[l], peer_u[l], peer_v[l])
    return x
```

```python
import numpy as np
import ml_dtypes
from contextlib import ExitStack
import concourse.bass as bass
import concourse.mybir as mybir
from concourse.bass_utils import run_bass_kernel_spmd

F32 = mybir.dt.float32
BF16 = mybir.dt.bfloat16
U32 = mybir.dt.uint32
ALU = mybir.AluOpType
AF = mybir.ActivationFunctionType
AX = mybir.AxisListType
NPBF = ml_dtypes.bfloat16

NCORES = 8
SEQ = 2048
D = 1024
NTOK = 2 * SEQ
EPS = 1e-6
LAM_INIT = 0.8 - 0.6 * 1.0
PT = 256


class Buf:
    __slots__ = ("name", "w", "r", "dsem", "dcount")

    def __init__(self, name):
        self.name = name
        self.w = None
        self.r = []
        self.dsem = None
        self.dcount = 0


class Sched:
    ENG = ("pe", "act", "dve", "pool", "sp")

    def __init__(self, nc, stack):
        self.nc = nc
        self.stack = stack
        self.items = {e: [] for e in self.ENG}
        self.count = {e: 0 for e in self.ENG}
        self.known = {e: {} for e in self.ENG}
        self.sems = {}
        self.nsem = 0
        self.bufs = []
        for e in ("pe", "act", "dve", "pool"):
            self._sem("eng_" + e)

    def buf(self, name):
        b = Buf(name)
        self.bufs.append(b)
        return b

    def _sem(self, key):
        if key not in self.sems:
            self.sems[key] = self.stack.enter_context(self.nc.semaphore("s%d" % self.nsem))
            self.nsem += 1
        return key

    def _collect(self, eng, reads, writes):
        need = {}

        def add(ev):
            if ev is None:
                return
            k, v = ev
            if eng == "pe" and k == "eng_pe":
                return
            if need.get(k, 0) < v:
                need[k] = v

        for b in reads:
            add(b.w)
        for b in writes:
            add(b.w)
            for ev in b.r:
                add(ev)
        kn = self.known[eng]
        waits = []
        for k, v in need.items():
            if kn.get(k, 0) >= v:
                continue
            kn[k] = v
            waits.append((k, v))
        return waits

    def _post(self, ev, reads, writes):
        for b in reads:
            if len(b.r) > 6:
                m = {}
                for k, v in b.r:
                    if m.get(k, 0) < v:
                        m[k] = v
                b.r = list(m.items())
            b.r.append(ev)
        for b in writes:
            b.w = ev
            b.r = []

    def op(self, eng, fn, reads=(), writes=()):
        waits = self._collect(eng, reads, writes)
        self.count[eng] += 1
        ev = ("eng_" + eng, self.count[eng])
        self.items[eng].append((waits, fn, ev[0], 1))
        self._post(ev, reads, writes)
        return ev

    def dma(self, fn, reads=(), writes=(), sembuf=None, eng="sp"):
        waits = self._collect(eng, reads, writes)
        if sembuf is None:
            sembuf = writes[0] if writes else reads[0]
        if sembuf.dsem is None:
            sembuf.dsem = self._sem("dma%d" % self.nsem)
        sembuf.dcount += 1
        ev = (sembuf.dsem, 16 * sembuf.dcount)
        self.items[eng].append((waits, fn, ev[0], 16))
        self._post(ev, reads, writes)
        return ev

    def wait_all(self, eng, bufs):
        waits = self._collect(eng, (), bufs)
        if waits:
            self.items[eng].append((waits, None, None, 0))

    def barrier(self):
        for e in self.ENG:
            self.wait_all(e, self.bufs)

    def replay(self):
        nc = self.nc
        sems = self.sems
        items = self.items
        with nc.Block() as block:
            def run(e):
                def body(eng):
                    for waits, fn, sk, amt in items[e]:
                        for k, v in waits:
                            eng.wait_ge(sems[k], v)
                        if fn is not None:
                            fn(eng).then_inc(sems[sk], amt)
                return body
            block.sync(run("sp"))
            block.tensor(run("pe"))
            block.vector(run("dve"))
            block.scalar(run("act"))
            block.gpsimd(run("pool"))


class Arena:
    def __init__(self, t, nwords):
        self.t = t
        self.n = nwords
        self.off = 0

    def mark(self):
        return self.off

    def reset(self, m):
        self.off = m

    def alloc(self, shape, dt):
        n = 1
        for s in shape:
            n *= s
        words = n if dt in (F32, U32) else (n + 1) // 2
        a = self.off
        self.off += words
        assert self.off <= self.n, ("SBUF arena overflow", self.off, self.n, shape, a)
        v = self.t[:, a:a + words]
        if dt != F32:
            v = v.bitcast(dt)
            if dt == BF16 and n % 2:
                v = v[:, 0:n]
        if len(shape) == 2:
            return v.rearrange("p (a b) -> p a b", a=shape[0])
        if len(shape) == 3:
            return v.rearrange("p (a b c) -> p a b c", a=shape[0], b=shape[1])
        return v


WLIST = [("w_in_r", 8192), ("wv_r", 1024), ("wap_r", 1024), ("wcp_r", 1024), ("wout_r", 1024),
         ("wq_r", 2048), ("sk_r", 256), ("uv_r", 32768)]


def build_program(dbg=None):
    nc = bass.Bass("TRN2", target_bir_lowering=False)
    din = lambda n, s, dt=F32: nc.dram_tensor(n, s, dt, kind="ExternalInput").ap()
    x = din("x", [NTOK, D])
    wf = {n: din(n, [r, 1024]) for n, r in WLIST}
    wb = {n: nc.dram_tensor(n + "_b", [r, 1024], BF16, kind="Internal").ap() for n, r in WLIST}
    x2d = nc.dram_tensor("x2scr", [NTOK, D], F32, kind="Internal").ap()
    gmix = din("gmix", [D]); gffn = din("gffn", [D])
    qng = din("qng", [64]); kng = din("kng", [64])
    lams = din("lams", [4, 64])
    gsub = din("gsub", [128])
    cwd = din("cw", [128, 24])
    identb_d = din("identb", [128, 128], BF16); identf_d = din("identf", [128, 128])
    tri_d = din("tri", [128, 128], BF16); blk_d = din("blk64", [128, 128], BF16)
    iota_d = din("iota128", [128, 128], BF16); iota16_d = din("iota16", [128, 16])
    qpos_d = din("qpos", [8, 4, SEQ], BF16); kpos_d = din("kpos", [8, 4, SEQ], BF16)
    out = nc.dram_tensor("out", [NTOK, D], F32, kind="ExternalOutput").ap()
    dbg_out = {}
    if dbg:
        for n, shp in dbg.items():
            dbg_out[n] = nc.dram_tensor("dbg_" + n, list(shp), F32, kind="ExternalOutput").ap()

    with ExitStack() as st:
        S = Sched(nc, st)
        AW = 51 * 1024
        arena_t = st.enter_context(nc.sbuf_tensor("arena", [128, AW], F32))
        A = Arena(arena_t, AW)
        PS = [st.enter_context(nc.psum_tensor("ps%d" % i, [128, 512], F32))[:] for i in range(8)]
        BP = [S.buf("ps%d" % i) for i in range(8)]

        def MM(o, lhsT, rhs, start, stop, reads, writes):
            S.op("pe", lambda e: e.matmul(o, lhsT=lhsT, rhs=rhs, start=start, stop=stop), reads, writes)

        def TR(o, in_, ident, reads, writes):
            S.op("pe", lambda e: e.transpose(out=o, in_=in_, identity=ident), reads, writes)

        def ACT(o, in_, func, reads, writes, **kw):
            S.op("act", lambda e: e.activation(out=o, in_=in_, func=func, **kw), reads, writes)

        def TT(eng, o, in0, in1, op, reads, writes):
            S.op(eng, lambda e: e.tensor_tensor(out=o, in0=in0, in1=in1, op=op), reads, writes)

        def TS(eng, o, in0, s1, s2, op0, op1, reads, writes):
            if s2 is None:
                S.op(eng, lambda e: e.tensor_scalar(out=o, in0=in0, scalar1=s1, scalar2=None, op0=op0), reads, writes)
            else:
                S.op(eng, lambda e: e.tensor_scalar(out=o, in0=in0, scalar1=s1, scalar2=s2, op0=op0, op1=op1), reads, writes)

        def STT(eng, o, in0, sc, in1, op0, op1, reads, writes):
            S.op(eng, lambda e: e.scalar_tensor_tensor(out=o, in0=in0, scalar=sc, in1=in1, op0=op0, op1=op1), reads, writes)

        def CP(eng, o, in_, reads, writes):
            if eng == "act":
                ACT(o, in_, AF.Copy, reads, writes)
            else:
                S.op(eng, lambda e: e.tensor_copy(out=o, in_=in_), reads, writes)

        def MSET(eng, o, val, writes):
            S.op(eng, lambda e: e.memset(o, val), (), writes)

        def DMA(o, in_, reads, writes, sembuf=None):
            S.dma(lambda e: e.dma_start(out=o, in_=in_), reads, writes, sembuf)

        def RSTD(o, ss, n, rb, wbuf, tmp):
            ACT(tmp, ss, AF.Ln, rb, [wbuf], bias=EPS, scale=1.0 / n)
            ACT(o, tmp, AF.Exp, [wbuf], [wbuf], scale=-0.5)

        identb = A.alloc([128], BF16); identf = A.alloc([128], F32)
        tri = A.alloc([128], BF16); blk64 = A.alloc([128], BF16)
        iota = A.alloc([128], BF16); iota16 = A.alloc([16], F32)
        gq = A.alloc([1], F32); gk = A.alloc([1], F32)
        neglam = A.alloc([1], F32)
        gsubr = A.alloc([128], F32)
        cw = A.alloc([24], F32)
        grep = A.alloc([1024], F32)
        lamt = A.alloc([4, 64], F32); lamw = A.alloc([8], F32)
        Bc = S.buf("consts")
        Bgrep = S.buf("grep")
        DMA(identb, identb_d[:, :], [], [Bc]); DMA(identf, identf_d[:, :], [], [Bc])
        DMA(tri, tri_d[:, :], [], [Bc]); DMA(blk64, blk_d[:, :], [], [Bc])
        DMA(iota, iota_d[:, :], [], [Bc]); DMA(iota16, iota16_d[:, :], [], [Bc])
        DMA(cw, cwd[:, :], [], [Bc])
        DMA(gsubr, gsub.partition_broadcast(128), [], [Bc])
        for j in range(4):
            DMA(lamt[:, j, :], lams[j, :].partition_broadcast(128), [], [Bc])
        q1 = qng.rearrange("(d o) -> d o", o=1); k1 = kng.rearrange("(d o) -> d o", o=1)
        DMA(gq[0:64, :], q1, [], [Bc]); DMA(gq[64:128, :], q1, [], [Bc])
        DMA(gk[0:64, :], k1, [], [Bc]); DMA(gk[64:128, :], k1, [], [Bc])
        TS("dve", gq, gq, 0.125, None, ALU.mult, None, [Bc], [Bc])
        TS("dve", gsubr, gsubr, 1.0 - LAM_INIT, None, ALU.mult, None, [Bc], [Bc])
        TT("dve", lamt[:, 0, :], lamt[:, 0, :], lamt[:, 1, :], ALU.mult, [Bc], [Bc])
        TT("dve", lamt[:, 2, :], lamt[:, 2, :], lamt[:, 3, :], ALU.mult, [Bc], [Bc])
        S.op("dve", lambda e: e.tensor_reduce(out=lamw[:, 0:1], in_=lamt[:, 0, :], axis=AX.X, op=ALU.add), [Bc], [Bc])
        S.op("dve", lambda e: e.tensor_reduce(out=lamw[:, 1:2], in_=lamt[:, 2, :], axis=AX.X, op=ALU.add), [Bc], [Bc])
        ACT(lamw[:, 2:4], lamw[:, 0:2], AF.Exp, [Bc], [Bc])
        TT("dve", lamw[:, 4:5], lamw[:, 3:4], lamw[:, 2:3], ALU.subtract, [Bc], [Bc])
        TS("dve", neglam, lamw[:, 4:5], -LAM_INIT, None, ALU.add, None, [Bc], [Bc])
        base_mark = A.mark()

        NSL = 6
        cin = [A.alloc([2, 1024], F32) for _ in range(NSL)]
        cout = [A.alloc([2, 1024], BF16) for _ in range(NSL)]
        Bci = [S.buf("cin%d" % i) for i in range(NSL)]
        Bco = [S.buf("cout%d" % i) for i in range(NSL)]
        Bw = {n: S.buf("w_" + n) for n, _ in WLIST}
        jobs = []
        for n, rows in WLIST:
            if n == "uv_r":
                continue
            src = wf[n].rearrange("(s b p) c -> s p b c", b=2, p=128)
            dst = wb[n].rearrange("(s b p) c -> s p b c", b=2, p=128)
            for s_ in range(rows // 256):
                jobs.append((n, src[s_], dst[s_]))
        PD = 4
        for j in range(min(PD, len(jobs))):
            DMA(cin[j % NSL], jobs[j][1], [], [Bci[j % NSL]])
        for j, (n, src_, dst_) in enumerate(jobs):
            k = j % NSL
            if j + PD < len(jobs):
                DMA(cin[(j + PD) % NSL], jobs[j + PD][1], [], [Bci[(j + PD) % NSL]])
            CP(("act", "dve", "act", "dve", "pool")[j % 5], cout[k], cin[k], [Bci[k]], [Bco[k]])
            DMA(dst_, cout[k], [Bco[k]], [Bw[n]], sembuf=Bco[k])
        S.barrier()
        A.reset(base_mark)

        w_in_c = wb["w_in_r"].rearrange("(c p) n -> c p n", p=128)
        wap_c = wb["wap_r"].rearrange("(c p) n -> c p n", p=128)
        wcp_c = wb["wcp_r"].rearrange("(c p) n -> c p n", p=128)
        hT = A.alloc([8, SEQ], BF16); BhT = S.buf("hT")
        vtok = A.alloc([16 * 8, 129], BF16); Bv = S.buf("vtok")
        vtok4 = vtok.rearrange("p (t h) e -> p t h e", h=8)
        mixedT = vtok.rearrange("p a e -> p (a e)")[:, 0:8 * SEQ].rearrange("p (c t) -> p c t", c=8)
        attnT = A.alloc([8, SEQ], BF16); BaT = S.buf("attnT")
        convT = A.alloc([8, SEQ], BF16); BcT = S.buf("convT")
        mix_mark = A.mark()
        Bx2d = S.buf("x2d")

        def proj(psb, Bps, wt, Bwt, rhs_of_kc, Brhs):
            for kc in range(8):
                MM(psb, wt[:, kc, :], rhs_of_kc(kc), kc == 0, kc == 7, [Bwt, Brhs], [Bps])

        bg = {"done": 0}
        BG_N = 32768 // 128
        uv_src = wf["uv_r"].rearrange("(s p) c -> s p c", p=128)
        uv_dst = wb["uv_r"].rearrange("(s p) c -> s p c", p=128)

        def bg_step(bin_, bout_, Bbi, Bbo):
            j = bg["done"]
            if j >= BG_N:
                return
            k = j % 2
            if j == 0:
                S.dma(lambda e: e.dma_start(out=bin_[0], in_=uv_src[0]), [], [Bbi[0]], eng="pool")
            if j + 1 < BG_N:
                kn = (j + 1) % 2
                S.dma(lambda e: e.dma_start(out=bin_[kn], in_=uv_src[j + 1]), [], [Bbi[kn]], eng="pool")
            CP("dve", bout_[k], bin_[k], [Bbi[k]], [Bbo[k]])
            S.dma(lambda e: e.dma_start(out=uv_dst[j], in_=bout_[k]), [Bbo[k]], [Bw["uv_r"]], Bbo[k], eng="pool")
            bg["done"] = j + 1

        for sq_ in range(2):
            tok0 = sq_ * SEQ
            A.reset(mix_mark)
            xin = [A.alloc([1024], F32) for _ in range(2)]; Bx = [S.buf("xin%d" % i) for i in range(2)]
            hb = [A.alloc([1024], BF16) for _ in range(2)]; Bhb = [S.buf("hb%d" % i) for i in range(2)]
            junk = A.alloc([1024], BF16); Bj = S.buf("junk")
            st1 = [A.alloc([4], F32) for _ in range(2)]; Bs1 = [S.buf("st%d" % i) for i in range(2)]
            wbig = A.alloc([8, 1024], BF16); Bwbig = S.buf("wbig")
            if sq_ == 0:
                DMA(grep, gmix.partition_broadcast(128), [], [Bgrep])
            DMA(wbig.rearrange("p a b -> p (a b)"), wb["wv_r"].rearrange("(p a) c -> p (a c)", p=128), [Bw["wv_r"]], [Bwbig])
            for tt in range(16):
                k = tt % 2
                DMA(xin[k], x[tok0 + tt * 128: tok0 + (tt + 1) * 128, :], [], [Bx[k]])
                MSET("pool", st1[k][:, 0:1], 0.0, [Bs1[k]])
                ACT(junk, xin[k], AF.Square, [Bx[k]], [Bj, Bs1[k]], accum_out=st1[k][:, 0:1])
                RSTD(st1[k][:, 2:3], st1[k][:, 0:1], 1024.0, [Bs1[k]], Bs1[k], st1[k][:, 1:2])
                STT("dve", hb[k], xin[k], st1[k][:, 2:3], grep, ALU.mult, ALU.mult, [Bx[k], Bs1[k], Bgrep], [Bhb[k]])
                pb = PS[k].bitcast(BF16)
                for kc in range(8):
                    TR(pb[:, kc * 128:(kc + 1) * 128], hb[k][:, kc * 128:(kc + 1) * 128], identb, [Bhb[k], Bc], [BP[k]])
                CP("act" if k else "dve", hT[:, :, tt * 128:(tt + 1) * 128],
                   pb[:, 0:1024].rearrange("p (c t) -> p c t", c=8), [BP[k]], [BhT])
            MSET("pool", vtok[:, :, 128:129], 1.0, [Bv])
            for tt in range(16):
                for half in range(2):
                    b_ = 2 + (tt * 2 + half) % 4
                    for kc in range(8):
                        MM(PS[b_], hT[:, kc, tt * 128:(tt + 1) * 128], wbig[:, kc, half * 512:(half + 1) * 512],
                           kc == 0, kc == 7, [BhT, Bwbig], [BP[b_]])
                    CP("act" if half else "dve", vtok4[:, tt, 4 * half:4 * half + 4, 0:128],
                       PS[b_].rearrange("p (h e) -> p h e", h=4), [BP[b_]], [Bv])
            S.barrier()
            A.reset(mix_mark)
            QA = [A.alloc([SEQ], BF16) for _ in range(2)]; QB = [A.alloc([SEQ], BF16) for _ in range(2)]
            KA = [A.alloc([SEQ], BF16) for _ in range(2)]; KB = [A.alloc([SEQ], BF16) for _ in range(2)]
            BQ = [S.buf("Q%d" % i) for i in range(2)]; BK = [S.buf("K%d" % i) for i in range(2)]
            Bpos = [S.buf("pos%d" % i) for i in range(2)]
            wqk = [A.alloc([8, 128], BF16) for _ in range(2)]; Bwqk = [S.buf("wqk%d" % i) for i in range(2)]
            qf = [A.alloc([512], F32) for _ in range(2)]; Bqf = [S.buf("qf%d" % i) for i in range(2)]
            sqb = [A.alloc([512], BF16) for _ in range(2)]; Bsq = [S.buf("sq%d" % i) for i in range(2)]
            rsb = [A.alloc([512], F32) for _ in range(2)]; Brs = [S.buf("rs%d" % i) for i in range(2)]
            lnb = rsb
            NET = 4
            et = [A.alloc([512], BF16) for _ in range(NET)]; Bet = [S.buf("et%d" % i) for i in range(NET)]
            sm = [A.alloc([8], F32) for _ in range(2)]; Bsm = [S.buf("sm%d" % i) for i in range(2)]
            a0 = [A.alloc([128], F32) for _ in range(2)]; at_ = [A.alloc([128], F32) for _ in range(2)]
            an = [A.alloc([128], BF16) for _ in range(2)]
            Ba0 = [S.buf("a0%d" % i) for i in range(2)]
            junk2 = A.alloc([128], BF16); Bj2 = S.buf("junk2")
            gcnt = {"g": 0}
            SBK = (3, 4, 7)
            bg_in = [A.alloc([1024], F32) for _ in range(2)]; bg_out = [A.alloc([1024], BF16) for _ in range(2)]
            Bbg_i = [S.buf("bgi%d_%d" % (sq_, i)) for i in range(2)]; Bbg_o = [S.buf("bgo%d_%d" % (sq_, i)) for i in range(2)]
            if sq_ == 1 and bg["done"] < BG_N and bg["done"] > 0:
                jn = bg["done"]
                S.dma(lambda e: e.dma_start(out=bg_in[jn % 2], in_=uv_src[jn]), [], [Bbg_i[jn % 2]], eng="pool")

            def proj_steps(h):
                par = h % 2
                steps = []

                def s_pos():
                    DMA(QA[par][64:68, :], qpos_d[h], [], [Bpos[par]]); DMA(KA[par][64:68, :], kpos_d[h], [], [Bpos[par]])
                    DMA(QB[par][64:68, :], qpos_d[h], [], [Bpos[par]]); DMA(KB[par][64:68, :], kpos_d[h], [], [Bpos[par]])
                steps.append(s_pos)

                def s_w(which):
                    def f():
                        DMA(wqk[which].rearrange("p a b -> p (a b)"), w_in_c[which * 8 + h], [Bw["w_in_r"]], [Bwqk[which]])
                    return f

                def s_groupA(which, g):
                    def f():
                        for p in range(2):
                            for kc in range(8):
                                MM(PS[p][0:64, :], wqk[which][:, kc, p * 64:(p + 1) * 64], hT[:, kc, g * 512:(g + 1) * 512],
                                   kc == 0, kc == 7, [Bwqk[which], BhT], [BP[p]])
                            CP("act", qf[p][0:64, :], PS[p][0:64, :], [BP[p]], [Bqf[p]])
                            ACT(sqb[p][0:64, :], PS[p][0:64, :], AF.Square, [BP[p]], [Bsq[p]])
                    return f

                def s_groupB(which, g):
                    def f():
                        TA, TB, Bt, gv = (QA[par], QB[par], BQ[par], gq) if which == 0 else (KA[par], KB[par], BK[par], gk)
                        cs = slice(g * 512, (g + 1) * 512)
                        for p in range(2):
                            MM(PS[2][0:64, :], blk64[0:64, 0:64], sqb[p][0:64, :], True, True, [Bc, Bsq[p]], [BP[2]])
                            ACT(lnb[p][0:64, :], PS[2][0:64, :], AF.Ln, [BP[2]], [Brs[p]], bias=EPS, scale=1.0 / 64)
                            ACT(rsb[p][0:64, :], lnb[p][0:64, :], AF.Exp, [Brs[p]], [Brs[p]], scale=-0.5)
                            STT("dve", (TA, TB)[p][0:64, cs], qf[p][0:64, :], gv[0:64, 0:1], rsb[p][0:64, :], ALU.mult, ALU.mult,
                                [Bqf[p], Brs[p], Bc], [Bt])
                    return f
                for which in range(2):
                    steps.append(s_w(which))
                for which in range(2):
                    for g in range(4):
                        steps.append(s_groupA(which, g))
                        steps.append(s_groupB(which, g))
                return steps

            def attn_steps(h):
                par = h % 2
                qa, qb, ka, kb_ = QA[par], QB[par], KA[par], KB[par]
                groups = []
                for i in range(16):
                    for p in range(2):
                        for g0 in range(0, i + 1, 4):
                            groups.append((i, p, list(range(g0, min(g0 + 4, i + 1)))))

                def S_(gi):
                    def f():
                        i, p, kbs = groups[gi]
                        qs = slice(i * 128, (i + 1) * 128)
                        sb_ = SBK[gi % 3]
                        ek = gi % NET
                        for j, kb in enumerate(kbs):
                            ks = slice(kb * 128, (kb + 1) * 128)
                            o = PS[sb_][:, j * 128:(j + 1) * 128]
                            if p == 0:
                                MM(o, ka[0:68, ks], qa[0:68, qs], True, True, [BK[par], BQ[par], Bpos[par]], [BP[sb_]])
                            else:
                                MM(o, kb_[0:68, ks], qb[0:68, qs], True, True, [BK[par], BQ[par], Bpos[par]], [BP[sb_]])
                        n = len(kbs) * 128
                        ACT(et[ek][:, 0:n], PS[sb_][:, 0:n], AF.Exp, [BP[sb_]], [Bet[ek]])
                        if kbs[-1] == i:
                            j = len(kbs) - 1
                            TT("dve", et[ek][:, j * 128:(j + 1) * 128], et[ek][:, j * 128:(j + 1) * 128], tri, ALU.mult,
                               [Bet[ek], Bc], [Bet[ek]])
                    return f

                def A_(gi):
                    def f():
                        i, p, kbs = groups[gi]
                        ek = gi % NET
                        ub = 5 + i % 2
                        U = PS[ub][:, p * 256:p * 256 + 129]
                        for j, kb in enumerate(kbs):
                            MM(U, et[ek][:, j * 128:(j + 1) * 128], vtok4[:, kb, h, :], kb == 0, kb == i, [Bet[ek], Bv], [BP[ub]])
                        if p == 1 and kbs[-1] == i:
                            k = i % 2
                            U0 = PS[ub][:, 0:129]; U1 = PS[ub][:, 256:385]
                            S.op("dve", lambda e: e.reciprocal(out=sm[k][:, 0:1], in_=U0[:, 128:129]), [BP[ub]], [Bsm[k]])
                            S.op("dve", lambda e: e.reciprocal(out=sm[k][:, 1:2], in_=U1[:, 128:129]), [BP[ub]], [Bsm[k]])
                            TT("dve", sm[k][:, 2:3], sm[k][:, 1:2], neglam, ALU.mult, [Bsm[k], Bc], [Bsm[k]])
                            TS("dve", a0[k], U0[:, 0:128], sm[k][:, 0:1], None, ALU.mult, None, [BP[ub], Bsm[k]], [Ba0[k]])
                            STT("dve", at_[k], U1[:, 0:128], sm[k][:, 2:3], a0[k], ALU.mult, ALU.add, [BP[ub], Bsm[k], Ba0[k]], [Ba0[k]])
                            MSET("dve", sm[k][:, 3:4], 0.0, [Bsm[k]])
                            ACT(junk2, at_[k], AF.Square, [Ba0[k]], [Bj2, Bsm[k]], accum_out=sm[k][:, 3:4])
                            RSTD(sm[k][:, 5:6], sm[k][:, 3:4], 128.0, [Bsm[k]], Bsm[k], sm[k][:, 4:5])
                            STT("dve", an[k], at_[k], sm[k][:, 5:6], gsubr, ALU.mult, ALU.mult, [Ba0[k], Bsm[k], Bc], [Ba0[k]])
                    return f

                def T_(i):
                    def f():
                        k = i % 2
                        pT = PS[2].bitcast(BF16)[:, (i % 4) * 128:(i % 4 + 1) * 128]
                        TR(pT, an[k], identb, [Ba0[k], Bc], [BP[2]])
                        CP("act", attnT[:, h, i * 128:(i + 1) * 128], pT, [BP[2]], [BaT])
                    return f
                steps = []
                ng = len(groups)
                LA = 2
                for gi in range(min(LA, ng)):
                    steps.append(S_(gi))
                pend = []
                for gi in range(ng):
                    if gi + LA < ng:
                        steps.append(S_(gi + LA))
                    steps.append(A_(gi))
                    pend = [(i_, c_ - 1) for (i_, c_) in pend]
                    for (i_, c_) in pend:
                        if c_ <= 0:
                            steps.append(T_(i_))
                    pend = [(i_, c_) for (i_, c_) in pend if c_ > 0]
                    i, p, kbs = groups[gi]
                    if p == 1 and kbs[-1] == i:
                        pend.append((i, 3))
                for (i_, c_) in pend:
                    steps.append(T_(i_))
                return steps

            for f in proj_steps(0):
                f()
            for h in range(8):
                ast = attn_steps(h)
                pst = proj_steps(h + 1) if h + 1 < 8 else []
                na, npj = len(ast), len(pst)
                done = 0
                for j, f in enumerate(ast):
                    f()
                    if j % 5 == 2:
                        bg_step(bg_in, bg_out, Bbg_i, Bbg_o)
                    tgt = (npj * (j + 1)) // na
                    while done < tgt:
                        pst[done]()
                        done += 1
            S.barrier()
            A.reset(mix_mark)
            wc3 = [[A.alloc([8, 128], BF16) for _ in range(3)] for _ in range(2)]
            Bwc3 = [S.buf("wc3%d" % i) for i in range(2)]
            ybuf = A.alloc([SEQ + 2], F32); By = S.buf("ybuf")
            ccs = [A.alloc([512], F32) for _ in range(2)]; zb = [A.alloc([512], F32) for _ in range(2)]
            Bcs = [S.buf("ccs%d" % i) for i in range(2)]
            MSET("pool", ybuf[:, 0:2], 0.0, [By])
            gi = 0
            for c in range(8):
                ws = c % 2
                for j3, cc in enumerate((24 + c, 32 + c, 40 + c)):
                    DMA(wc3[ws][j3].rearrange("p a b -> p (a b)"), w_in_c[cc], [Bw["w_in_r"]], [Bwc3[ws]])
                for g in range(4):
                    k = gi % 2
                    gi += 1
                    b0 = 3 * k
                    rhs = lambda kc, g=g: hT[:, kc, g * 512:(g + 1) * 512]
                    proj(PS[b0], BP[b0], wc3[ws][1], Bwc3[ws], rhs, BhT)
                    proj(PS[b0 + 1], BP[b0 + 1], wc3[ws][2], Bwc3[ws], rhs, BhT)
                    proj(PS[b0 + 2], BP[b0 + 2], wc3[ws][0], Bwc3[ws], rhs, BhT)
                    CP("act", ccs[k], PS[b0], [BP[b0]], [Bcs[k]])
                    c0 = g * 512
                    TT("dve", ybuf[:, 2 + c0:2 + c0 + 512], ccs[k], PS[b0 + 1], ALU.mult, [Bcs[k], BP[b0 + 1]], [By])
                    TS("dve", zb[k], ybuf[:, 2 + c0:2 + c0 + 512], cw[:, c * 3 + 2:c * 3 + 3], None, ALU.mult, None, [By, Bc], [Bcs[k]])
                    STT("dve", zb[k], ybuf[:, 1 + c0:1 + c0 + 512], cw[:, c * 3 + 1:c * 3 + 2], zb[k], ALU.mult, ALU.add, [By, Bc, Bcs[k]], [Bcs[k]])
                    STT("dve", zb[k], ybuf[:, c0:c0 + 512], cw[:, c * 3:c * 3 + 1], zb[k], ALU.mult, ALU.add, [By, Bc, Bcs[k]], [Bcs[k]])
                    TT("dve", convT[:, c, c0:c0 + 512], zb[k], PS[b0 + 2], ALU.mult, [Bcs[k], BP[b0 + 2]], [BcT])
            S.barrier()
            A.reset(mix_mark)
            w4 = [[A.alloc([8, 128], BF16) for _ in range(4)] for _ in range(2)]
            Bw4 = [S.buf("w4%d" % i) for i in range(2)]
            sg = [[A.alloc([512], F32) for _ in range(4)] for _ in range(2)]
            Bsg = [S.buf("sg%d" % i) for i in range(2)]
            gi = 0
            for c in range(8):
                ws = c % 2
                srcs = (wap_c[c], wcp_c[c], w_in_c[48 + c], w_in_c[56 + c])
                deps = (Bw["wap_r"], Bw["wcp_r"], Bw["w_in_r"], Bw["w_in_r"])
                for j4 in range(4):
                    DMA(w4[ws][j4].rearrange("p a b -> p (a b)"), srcs[j4], [deps[j4]], [Bw4[ws]])
                for g in range(4):
                    k = gi % 2
                    gi += 1
                    b0 = 4 * k
                    gsl = slice(g * 512, (g + 1) * 512)
                    proj(PS[b0], BP[b0], w4[ws][0], Bw4[ws], lambda kc: attnT[:, kc, gsl], BaT)
                    proj(PS[b0 + 1], BP[b0 + 1], w4[ws][1], Bw4[ws], lambda kc: convT[:, kc, gsl], BcT)
                    proj(PS[b0 + 2], BP[b0 + 2], w4[ws][2], Bw4[ws], lambda kc: hT[:, kc, gsl], BhT)
                    proj(PS[b0 + 3], BP[b0 + 3], w4[ws][3], Bw4[ws], lambda kc: hT[:, kc, gsl], BhT)
                    ACT(sg[k][0], PS[b0 + 2], AF.Sigmoid, [BP[b0 + 2]], [Bsg[k]])
                    ACT(sg[k][1], PS[b0 + 3], AF.Sigmoid, [BP[b0 + 3]], [Bsg[k]])
                    TT("dve", sg[k][2], sg[k][0], PS[b0], ALU.mult, [Bsg[k], BP[b0]], [Bsg[k]])
                    TT("dve", sg[k][3], sg[k][1], PS[b0 + 1], ALU.mult, [Bsg[k], BP[b0 + 1]], [Bsg[k]])
                    TT("pool", mixedT[:, c, gsl], sg[k][2], sg[k][3], ALU.add, [Bsg[k]], [Bv])
            S.barrier()
            A.reset(mix_mark)
            xin = [A.alloc([1024], F32) for _ in range(2)]; Bx = [S.buf("xin%d" % i) for i in range(2)]
            x2t = [A.alloc([1024], F32) for _ in range(2)]; Bx2 = [S.buf("x2t%d" % i) for i in range(2)]
            wbig = A.alloc([8, 1024], BF16); Bwbig = S.buf("wbig")
            DMA(wbig.rearrange("p a b -> p (a b)"), wb["wout_r"].rearrange("(p a) c -> p (a c)", p=128), [Bw["wout_r"]], [Bwbig])
            for tt in range(16):
                k = tt % 2
                DMA(xin[k], x[tok0 + tt * 128: tok0 + (tt + 1) * 128, :], [], [Bx[k]])
                for half in range(2):
                    b_ = (tt * 2 + half) % 4
                    hs = slice(half * 512, (half + 1) * 512)
                    for kc in range(8):
                        MM(PS[b_], mixedT[:, kc, tt * 128:(tt + 1) * 128], wbig[:, kc, hs], kc == 0, kc == 7, [Bv, Bwbig], [BP[b_]])
                    TT("dve", x2t[k][:, hs], xin[k][:, hs], PS[b_], ALU.add, [Bx[k], BP[b_]], [Bx2[k]])
                DMA(x2d[tok0 + tt * 128: tok0 + (tt + 1) * 128, :], x2t[k], [Bx2[k]], [Bx2d], sembuf=Bx2[k])
            S.barrier()

        A.reset(base_mark)
        DMA(grep, gffn.partition_broadcast(128), [], [Bgrep])
        NST = PT // 128
        NTILE = NTOK // PT
        wq_c = wb["wq_r"].rearrange("(c p) n -> c p n", p=128)
        uv_c = wb["uv_r"].rearrange("(i p two) n -> i p (two n)", p=128, two=2)
        skt = A.alloc([16, 128], BF16); Bsk = S.buf("sk")
        DMA(skt.rearrange("p a b -> p (a b)"), wb["sk_r"].rearrange("(p a) c -> p (a c)", p=128), [Bw["sk_r"]], [Bsk])
        GTh = [A.alloc([PT, 64], BF16) for _ in range(2)]; BGT = [S.buf("GT%d" % i) for i in range(2)]
        x2k = [A.alloc([NST, 1024], F32) for _ in range(2)]; Bx2k = [S.buf("x2k%d" % i) for i in range(2)]
        h2T = [A.alloc([8, PT], BF16) for _ in range(2)]; Bh2T = [S.buf("h2T%d" % i) for i in range(2)]
        qT = A.alloc([16, PT], BF16); BqT = S.buf("qT")
        hb1 = A.alloc([1024], BF16); Bhb1 = S.buf("hbp")
        st1 = [A.alloc([4], F32) for _ in range(2)]; Bs1 = [S.buf("stp%d" % i) for i in range(2)]
        NWQ = 4
        wqs = [A.alloc([8, 128], BF16) for _ in range(NWQ)]; Bwqs = [S.buf("wqs%d" % i) for i in range(NWQ)]
        s_sb = A.alloc([16, 128], F32); s_w = A.alloc([16, 128], F32); Bs = S.buf("s_sb"); Bsw = S.buf("s_w")
        sv = A.alloc([16, 16], F32); si = A.alloc([16, 16], U32); sif = A.alloc([16, 16], F32); Bsv = S.buf("sv")
        cand = A.alloc([8 * 16, 16], F32); candw = A.alloc([8 * 16, 16], F32); Bcand = S.buf("cand"); Bcw = S.buf("candw")
        tv = A.alloc([8, 16], F32); posu = A.alloc([8, 16], U32); r0u = posu; r1u = A.alloc([8, 16], U32)
        r0f = A.alloc([8, 16], F32); r1f = A.alloc([8, 16], F32); Btv = S.buf("tv")
        IJg = A.alloc([3, 128], F32); BIJ = S.buf("IJg")
        tvz = A.alloc([8, 2], F32)
        ITs = A.alloc([3, PT], F32); BIT = S.buf("ITs")
        JTb = A.alloc([PT], BF16); ITb = A.alloc([PT], BF16)
        HT = 16
        AfT = [A.alloc([HT, 64], BF16) for _ in range(2)]; BfT = [A.alloc([HT, 128], BF16) for _ in range(2)]
        BAf = [S.buf("Af%d" % i) for i in range(2)]; BBf = [S.buf("Bf%d" % i) for i in range(2)]
        NUV = 6
        uvt = [A.alloc([2048], BF16) for _ in range(NUV)]
        uv = [[t_[:, 0:1024], t_[:, 1024:2048]] for t_ in uvt]
        Buv = [S.buf("uv%d" % i) for i in range(NUV)]
        glb = [A.alloc([PT], BF16) for _ in range(5)]; wTb = [A.alloc([PT], BF16) for _ in range(5)]
        Bgl = [S.buf("gl%d" % i) for i in range(5)]
        BPA = [S.buf("pa%d" % i) for i in range(3)]
        Bout = S.buf("out")
        sv4 = sv.rearrange("p (h two) r -> p h two r", two=2)
        sif4 = sif.rearrange("p (h two) r -> p h two r", two=2)
        cand4 = cand.rearrange("p (h a) b -> p h a b", h=8)
        candw4 = candw.rearrange("p (h a) b -> p h a b", h=8)
        candf = cand.rearrange("p (h a) b -> p h (a b)", h=8)
        candwf = candw.rearrange("p (h a) b -> p h (a b)", h=8)
        IJg4 = IJg.rearrange("p w (h r) -> p w h r", h=8)
        PDMA = lambda o, in_, reads, writes, sembuf=None: S.dma(lambda e: e.dma_start(out=o, in_=in_), reads, writes, sembuf, eng="pool")
        cnt = {"wq": 0, "uv": 0, "gl": 0, "g": 0}

        def topk_steps(T):
            steps = []
            pb_ = T % 2
            t0 = T * PT
            H2 = h2T[pb_]; X2 = x2k[pb_]

            def s_load(a):
                def f():
                    k = a % 2
                    PDMA(X2[:, a, :], x2d[t0 + a * 128:t0 + (a + 1) * 128, :], [Bx2d], [Bx2k[pb_]])
                    MSET("pool", st1[k][:, 0:1], 0.0, [Bs1[k]])
                    ACT(hb1, X2[:, a, :], AF.Square, [Bx2k[pb_]], [Bhb1, Bs1[k]], accum_out=st1[k][:, 0:1])
                    RSTD(st1[k][:, 2:3], st1[k][:, 0:1], 1024.0, [Bs1[k]], Bs1[k], st1[k][:, 1:2])
                    STT("dve", hb1, X2[:, a, :], st1[k][:, 2:3], grep, ALU.mult, ALU.mult, [Bx2k[pb_], Bs1[k], Bgrep], [Bhb1])
                return f

            def s_loadB(a):
                def f():
                    k = a % 2
                    pb = PS[4 + k].bitcast(BF16)
                    for kc in range(8):
                        TR(pb[:, kc * 128:(kc + 1) * 128], hb1[:, kc * 128:(kc + 1) * 128], identb, [Bhb1, Bc], [BP[4 + k]])
                    CP("act", H2[:, :, a * 128:(a + 1) * 128], pb[:, 0:1024].rearrange("p (c t) -> p c t", c=8), [BP[4 + k]], [Bh2T[pb_]])
                return f

            def s_wq(c):
                def f():
                    ws = (cnt["wq"] + c) % NWQ
                    PDMA(wqs[ws].rearrange("p a b -> p (a b)"), wq_c[c], [Bw["wq_r"]], [Bwqs[ws]])
                return f

            def s_q(c):
                def f():
                    ws = (cnt["wq"] + c) % NWQ
                    b_ = 4 + c % 2
                    for kc in range(8):
                        MM(PS[b_][:, 0:PT], wqs[ws][:, kc, :], H2[:, kc, :], kc == 0, kc == 7, [Bwqs[ws], Bh2T[pb_]], [BP[b_]])
                    CP("act", qT[:, c, :], PS[b_][:, 0:PT], [BP[b_]], [BqT])
                    if c == 15:
                        cnt["wq"] += 16
                return f

            def s_scores(a, q4):
                def f():
                    asl = slice(a * 128, (a + 1) * 128)
                    b_ = 4 + q4 % 2
                    for j in range(4):
                        hp = q4 * 4 + j
                        MM(PS[b_][:, j * 128:(j + 1) * 128], qT[:, hp, asl], skt[:, hp, :], True, True, [BqT, Bsk], [BP[b_]])
                    CP("act", s_sb[:, q4 * 4:q4 * 4 + 4, :], PS[b_].rearrange("p (a b) -> p a b", a=4), [BP[b_]], [Bs])
                return f

            def s_top(hp):
                def f():
                    S.op("dve", lambda e: e.max(out=sv[:, hp, 0:8], in_=s_sb[:, hp, :]), [Bs], [Bsv])
                    S.op("dve", lambda e: e.max_index(out=si[:, hp, 0:8], in_max=sv[:, hp, 0:8], in_values=s_sb[:, hp, :]), [Bs, Bsv], [Bsv])
                    S.op("dve", lambda e: e.match_replace(out=s_w[:, hp, :], in_to_replace=sv[:, hp, 0:8], in_values=s_sb[:, hp, :], imm_value=-1e30), [Bs, Bsv], [Bsw])
                    S.op("dve", lambda e: e.max(out=sv[:, hp, 8:16], in_=s_w[:, hp, :]), [Bsw], [Bsv])
                    S.op("dve", lambda e: e.max_index(out=si[:, hp, 8:16], in_max=sv[:, hp, 8:16], in_values=s_w[:, hp, :]), [Bsw, Bsv], [Bsv])
                return f

            def s_cand():
                CP("dve", sif, si, [Bsv], [Bsv])
                TT("dve", cand4, sv4[:, :, 0, :].unsqueeze(3).to_broadcast([128, 8, 16, 16]),
                   sv4[:, :, 1, :].unsqueeze(2).to_broadcast([128, 8, 16, 16]), ALU.add, [Bsv], [Bcand])

            def s_top2(h):
                def f():
                    S.op("dve", lambda e: e.max(out=tv[:, h, 0:8], in_=candf[:, h, :]), [Bcand], [Btv])
                    S.op("dve", lambda e: e.max_index(out=posu[:, h, 0:8], in_max=tv[:, h, 0:8], in_values=candf[:, h, :]), [Bcand, Btv], [Btv])
                    S.op("dve", lambda e: e.match_replace(out=candwf[:, h, :], in_to_replace=tv[:, h, 0:8], in_values=candf[:, h, :], imm_value=-1e30), [Bcand, Btv], [Bcw])
                    S.op("dve", lambda e: e.max(out=tv[:, h, 8:16], in_=candwf[:, h, :]), [Bcw], [Btv])
                    S.op("dve", lambda e: e.max_index(out=posu[:, h, 8:16], in_max=tv[:, h, 8:16], in_values=candwf[:, h, :]), [Bcw, Btv], [Btv])
                return f

            def s_idx(a):
                def f():
                    asl = slice(a * 128, (a + 1) * 128)
                    S.op("dve", lambda e: e.tensor_single_scalar(out=r1u, in_=posu, scalar=15, op=ALU.bitwise_and), [Btv], [Btv])
                    S.op("dve", lambda e: e.tensor_single_scalar(out=r0u, in_=posu, scalar=4, op=ALU.logical_shift_right), [Btv], [Btv])
                    CP("dve", r0f, r0u, [Btv], [Btv]); CP("dve", r1f, r1u, [Btv], [Btv])
                    for w_, rf in ((0, r0f), (1, r1f)):
                        TT("dve", candw4, iota16.unsqueeze(1).unsqueeze(1).to_broadcast([128, 8, 16, 16]),
                           rf.unsqueeze(3).to_broadcast([128, 8, 16, 16]), ALU.is_equal, [Btv, Bc, Bcand], [Bcw])
                        TT("dve", candw4, candw4, sif4[:, :, w_, :].unsqueeze(2).to_broadcast([128, 8, 16, 16]), ALU.mult, [Bcw, Bsv], [Bcw])
                        S.op("dve", lambda e, w_=w_: e.tensor_reduce(out=IJg4[:, w_, :, :], in_=candw4, axis=AX.X, op=ALU.add), [Bcw], [BIJ])
                    TT("dve", r0f, tv, tv[:, :, 0:1].to_broadcast([128, 8, 16]), ALU.subtract, [Btv], [Btv])
                    ACT(r0f, r0f, AF.Exp, [Btv], [Btv])
                    S.op("dve", lambda e: e.tensor_reduce(out=tvz[:, :, 0], in_=r0f, axis=AX.X, op=ALU.add), [Btv], [Btv])
                    S.op("dve", lambda e: e.reciprocal(out=tvz[:, :, 1], in_=tvz[:, :, 0]), [Btv], [Btv])
                    TT("dve", IJg4[:, 2, :, :], r0f, tvz[:, :, 1:2].to_broadcast([128, 8, 16]), ALU.mult, [Btv], [BIJ])
                return f

            def s_idxB(a):
                def f():
                    asl = slice(a * 128, (a + 1) * 128)
                    for w_ in range(3):
                        TR(PS[4][:, w_ * 128:(w_ + 1) * 128], IJg[:, w_, :], identf, [BIJ, Bc], [BP[4]])
                    CP("act", ITs[:, :, asl], PS[4][:, 0:384].rearrange("p (w t) -> p w t", w=3), [BP[4]], [BIT])
                return f

            def s_final():
                CP("act", JTb, ITs[:, 1, :], [BIT], [BIT])
                CP("act", ITb, ITs[:, 0, :], [BIT], [BIT])

            wqsteps = [s_wq(c) for c in range(16)]
            steps.append(s_load(0)); steps.append(wqsteps[0]); steps.append(wqsteps[1]); steps.append(s_loadB(0))
            steps.append(s_load(1)); steps.append(wqsteps[2]); steps.append(s_loadB(1))
            for c in range(16):
                if c + 3 < 16:
                    steps.append(wqsteps[c + 3])
                steps.append(s_q(c))
            for q4 in range(4):
                steps.append(s_scores(0, q4))
            for hp in range(16):
                steps.append(s_top(hp))
            steps.append(s_cand)
            for h in range(8):
                steps.append(s_top2(h))
            steps.append(s_idx(0))
            for q4 in range(4):
                steps.append(s_scores(1, q4))
            steps.append(s_idxB(0))
            for hp in range(16):
                steps.append(s_top(hp))
            steps.append(s_cand)
            for h in range(8):
                steps.append(s_top2(h))
            steps.append(s_idx(1))
            steps.append(s_idxB(1))
            steps.append(s_final)
            return steps

        def cons_steps(T, half):
            steps = []
            G_ = GTh[half]

            def prod(hb_):
                def f():
                    k = hb_ % 2
                    ts = slice(hb_ * HT, (hb_ + 1) * HT)
                    TT("dve", BfT[k], iota.unsqueeze(1).to_broadcast([128, HT, 128]), JTb[:, ts].unsqueeze(2).to_broadcast([128, HT, 128]),
                       ALU.is_equal, [BIT, Bc], [BBf[k]])
                    TT("dve", AfT[k], iota[:, half * 64:(half + 1) * 64].unsqueeze(1).to_broadcast([128, HT, 64]),
                       ITb[:, ts].unsqueeze(2).to_broadcast([128, HT, 64]), ALU.is_equal, [BIT, Bc], [BAf[k]])
                    TT("dve", AfT[k], AfT[k], ITs[:, 2, ts].unsqueeze(2).to_broadcast([128, HT, 64]), ALU.mult, [BAf[k], BIT], [BAf[k]])
                return f

            def mm(hb_):
                def f():
                    k = hb_ % 2
                    for q8 in range(HT // 8):
                        b_ = 4 + q8 % 2
                        for j in range(8):
                            tl = q8 * 8 + j
                            MM(PS[b_][:, j * 64:(j + 1) * 64], BfT[k][:, tl, :], AfT[k][:, tl, :], True, True, [BAf[k], BBf[k]], [BP[b_]])
                        tg = hb_ * HT + q8 * 8
                        CP("act", G_[:, tg:tg + 8, :], PS[b_].rearrange("p (t i) -> p t i", t=8), [BP[b_]], [BGT[half]])
                return f
            nb = PT // HT
            steps.append(prod(0))
            for hb_ in range(nb):
                if hb_ + 1 < nb:
                    p_, m_ = prod(hb_ + 1), mm(hb_)
                    steps.append(lambda p_=p_, m_=m_: (p_(), m_()))
                else:
                    steps.append(mm(hb_))
            return steps

        def uv_load(g):
            if g >= NTILE * 128:
                return
            us = g % NUV
            i = g % 128
            DMA(uvt[us], uv_c[i], [Bw["uv_r"]], [Buv[us]])

        PF = NUV - 5
        NSLOT = 5

        def U_(g):
            T, i = divmod(g, 128)
            pb_ = T % 2
            uv_load(g + PF)
            us = g % NUV
            uT = uv[us][0].rearrange("p (a b) -> p a b", a=8)
            k = g % NSLOT
            pa = PS[6 + g % 2][:, 0:PT]
            bpa = BPA[g % 2]
            for kc in range(8):
                MM(pa, uT[:, kc, :], h2T[pb_][:, kc, :], kc == 0, kc == 7, [Buv[us], Bh2T[pb_]], [bpa])
            ACT(glb[k], pa, AF.Gelu, [bpa], [Bgl[k]])
            TT("pool", wTb[k], glb[k], GTh[i // 64][:, :, i % 64], ALU.mult, [Bgl[k], BGT[i // 64]], [Bgl[k]])

        def V_(g):
            T, i = divmod(g, 128)
            us = g % NUV
            k = g % NSLOT
            for a in range(NST):
                for hf in range(2):
                    MM(PS[a * 2 + hf], wTb[k][:, a * 128:(a + 1) * 128], uv[us][1][:, hf * 512:(hf + 1) * 512],
                       i == 0, i == 127, [Bgl[k], Buv[us]], [BP[a * 2 + hf]])

        def finish(T):
            pb_ = T % 2
            t0 = T * PT
            for a in range(NST):
                for hf in range(2):
                    hs = slice(hf * 512, (hf + 1) * 512)
                    TT("dve", x2k[pb_][:, a, hs], x2k[pb_][:, a, hs], PS[a * 2 + hf], ALU.add, [Bx2k[pb_], BP[a * 2 + hf]], [Bx2k[pb_]])
                PDMA(out[t0 + a * 128:t0 + (a + 1) * 128, :], x2k[pb_][:, a, :], [Bx2k[pb_]], [Bout], sembuf=Bx2k[pb_])

        for f in topk_steps(0) + cons_steps(0, 0):
            f()
        for g in range(PF):
            uv_load(g)
        NG = NTILE * 128
        LA = 4
        for g in range(LA):
            U_(g)
        for T in range(NTILE):
            c1 = cons_steps(T, 1)
            tk = topk_steps(T + 1) if T + 1 < NTILE else []
            c0 = cons_steps(T + 1, 0) if T + 1 < NTILE else []
            split = len(tk) - 29 if tk else 0
            st_a = c1 + tk[:split]
            st_b = tk[split:] + c0
            for (lo, steps) in ((0, st_a), (64, st_b)):
                ns = len(steps)
                done = 0
                for j in range(64):
                    g = T * 128 + lo + j
                    if g + LA < NG:
                        U_(g + LA)
                    V_(g)
                    tgt = min(ns, (ns * (j + 1)) // 55)
                    while done < tgt:
                        steps[done]()
                        done += 1
            finish(T)
        S.barrier()
        S.replay()
    return nc


def _consts():
    c = {}
    c["identb"] = np.eye(128).astype(NPBF)
    c["identf"] = np.eye(128, dtype=np.float32)
    kk = np.arange(128)
    c["tri"] = (kk[None, :] >= kk[:, None]).astype(NPBF)
    c["blk64"] = ((kk[:, None] // 64) == (kk[None, :] // 64)).astype(NPBF)
    c["iota128"] = np.tile(np.arange(128, dtype=np.float32), (128, 1)).astype(NPBF)
    c["iota16"] = np.tile(np.arange(16, dtype=np.float32), (128, 1))
    pos = np.arange(SEQ)
    qq = (pos % 128).astype(np.float32); qb = (pos // 128).astype(np.float32)
    qpos = np.zeros((8, 4, SEQ), np.float32); kpos = np.zeros((8, 4, SEQ), np.float32)
    for h in range(8):
        sl = 2.0 ** (-(h + 1))
        qpos[h, 0] = -sl * qq; qpos[h, 1] = -sl * 128.0 * qb; qpos[h, 2] = 1.0; qpos[h, 3] = 1.0
        kpos[h, 0] = 1.0; kpos[h, 1] = 1.0; kpos[h, 2] = sl * qq; kpos[h, 3] = sl * 128.0 * qb
    c["qpos"] = qpos.astype(NPBF); c["kpos"] = kpos.astype(NPBF)
    return c


def _layouts(w_in, w_attn_proj, w_conv_proj, w_out, peer_w_query, peer_sub_keys, peer_u, peer_v):
    f = lambda a: np.ascontiguousarray(a, dtype=np.float32)
    L = {}
    colchunk = lambda w, ncc: f(w.reshape(8, 128, ncc, 128).transpose(2, 1, 0, 3)).reshape(ncc * 128, 1024)
    rowmajor = lambda w: f(w.reshape(8, 128, w.shape[1]).transpose(1, 0, 2)).reshape(-1, 1024)
    L["w_in_r"] = colchunk(w_in, 64)
    L["wv_r"] = rowmajor(w_in[:, 2048:3072])
    L["wap_r"] = colchunk(w_attn_proj, 8)
    L["wcp_r"] = colchunk(w_conv_proj, 8)
    L["wout_r"] = rowmajor(w_out)
    L["wq_r"] = colchunk(peer_w_query, 16)
    L["sk_r"] = f(peer_sub_keys.reshape(16, 128, 128).transpose(2, 0, 1)).reshape(256, 1024)
    u_r = peer_u.reshape(128, 128, 8, 128).transpose(0, 3, 2, 1).reshape(128, 128, 1, 1024)
    v_r = peer_v.reshape(128, 128, 1, 1024)
    L["uv_r"] = f(np.concatenate([u_r, v_r], axis=2)).reshape(32768, 1024)
    return L


_NC_CACHE = {}


def kernel(x, norm_mix_g, w_in, q_norm_g, k_norm_g, lambda_q1, lambda_k1, lambda_q2, lambda_k2,
           subln_g, w_attn_proj, conv_w, w_conv_proj, w_out, norm_ffn_g,
           peer_w_query, peer_sub_keys, peer_u, peer_v, _dbg=None):
    A_ = lambda a: np.asarray(a)
    x = A_(x).astype(np.float32, copy=False)
    L = _layouts(A_(w_in)[0], A_(w_attn_proj)[0], A_(w_conv_proj)[0], A_(w_out)[0], A_(peer_w_query)[0],
                 A_(peer_sub_keys)[0], A_(peer_u)[0], A_(peer_v)[0])
    C = _consts()
    shared = dict(L)
    shared.update(C)
    shared["gmix"] = np.ascontiguousarray(A_(norm_mix_g)[0], np.float32)
    shared["gffn"] = np.ascontiguousarray(A_(norm_ffn_g)[0], np.float32)
    shared["qng"] = np.ascontiguousarray(A_(q_norm_g)[0], np.float32)
    shared["kng"] = np.ascontiguousarray(A_(k_norm_g)[0], np.float32)
    shared["lams"] = np.ascontiguousarray(np.stack([A_(lambda_q1)[0], A_(lambda_k1)[0], A_(lambda_q2)[0], A_(lambda_k2)[0]]), np.float32)
    shared["gsub"] = np.ascontiguousarray(A_(subln_g)[0], np.float32)
    shared["cw"] = np.ascontiguousarray(A_(conv_w)[0].reshape(3, 8, 128).transpose(2, 1, 0).reshape(128, 24), np.float32)
    key = repr(sorted(_dbg.items())) if _dbg else ""
    if key not in _NC_CACHE:
        _NC_CACHE[key] = build_program(_dbg)
    nc = _NC_CACHE[key]
    xs = x.reshape(NCORES, NTOK, D)
    in_maps = []
    for c in range(NCORES):
        m = dict(shared)
        m["x"] = np.ascontiguousarray(xs[c])
        in_maps.append(m)
    res = run_bass_kernel_spmd(nc, in_maps, core_ids=list(range(NCORES)))
    outs = np.stack([np.asarray(r["out"]) for r in res.results]).reshape(16, SEQ, D).astype(np.float32)
    if _dbg:
        return outs, res.results
    return outs
```

```python
import numpy as np
import ml_dtypes
from contextlib import ExitStack
import concourse.bass as bass
import concourse.mybir as mybir
from concourse.bass_utils import run_bass_kernel_spmd

F32 = mybir.dt.float32
BF16 = mybir.dt.bfloat16
U32 = mybir.dt.uint32
ALU = mybir.AluOpType
AF = mybir.ActivationFunctionType
AX = mybir.AxisListType
NPBF = ml_dtypes.bfloat16

NCORES = 8
SEQ = 2048
D = 1024
NTOK = 2 * SEQ
EPS = 1e-6
LAM_INIT = 0.8 - 0.6 * 1.0
PT = 256


class Buf:
    __slots__ = ("name", "w", "r", "dsem", "dcount")

    def __init__(self, name):
        self.name = name
        self.w = None
        self.r = []
        self.dsem = None
        self.dcount = 0


class Sched:
    ENG = ("pe", "act", "dve", "pool", "sp")

    def __init__(self, nc, stack):
        self.nc = nc
        self.stack = stack
        self.items = {e: [] for e in self.ENG}
        self.count = {e: 0 for e in self.ENG}
        self.known = {e: {} for e in self.ENG}
        self.sems = {}
        self.nsem = 0
        self.bufs = []
        for e in ("pe", "act", "dve", "pool"):
            self._sem("eng_" + e)

    def buf(self, name):
        b = Buf(name)
        self.bufs.append(b)
        return b

    def _sem(self, key):
        if key not in self.sems:
            self.sems[key] = self.stack.enter_context(self.nc.semaphore("s%d" % self.nsem))
            self.nsem += 1
        return key

    def _collect(self, eng, reads, writes):
        need = {}

        def add(ev):
            if ev is None:
                return
            k, v = ev
            if eng == "pe" and k == "eng_pe":
                return
            if need.get(k, 0) < v:
                need[k] = v

        for b in reads:
            add(b.w)
        for b in writes:
            add(b.w)
            for ev in b.r:
                add(ev)
        kn = self.known[eng]
        waits = []
        for k, v in need.items():
            if kn.get(k, 0) >= v:
                continue
            kn[k] = v
            waits.append((k, v))
        return waits

    def _post(self, ev, reads, writes):
        for b in reads:
            if len(b.r) > 6:
                m = {}
                for k, v in b.r:
                    if m.get(k, 0) < v:
                        m[k] = v
                b.r = list(m.items())
            b.r.append(ev)
        for b in writes:
            b.w = ev
            b.r = []

    def op(self, eng, fn, reads=(), writes=()):
        waits = self._collect(eng, reads, writes)
        self.count[eng] += 1
        ev = ("eng_" + eng, self.count[eng])
        self.items[eng].append((waits, fn, ev[0], 1))
        self._post(ev, reads, writes)
        return ev

    def dma(self, fn, reads=(), writes=(), sembuf=None, eng="sp"):
        waits = self._collect(eng, reads, writes)
        if sembuf is None:
            sembuf = writes[0] if writes else reads[0]
        if sembuf.dsem is None:
            sembuf.dsem = self._sem("dma%d" % self.nsem)
        sembuf.dcount += 1
        ev = (sembuf.dsem, 16 * sembuf.dcount)
        self.items[eng].append((waits, fn, ev[0], 16))
        self._post(ev, reads, writes)
        return ev

    def wait_all(self, eng, bufs):
        waits = self._collect(eng, (), bufs)
        if waits:
            self.items[eng].append((waits, None, None, 0))

    def barrier(self):
        for e in self.ENG:
            self.wait_all(e, self.bufs)

    def replay(self):
        nc = self.nc
        sems = self.sems
        items = self.items
        with nc.Block() as block:
            def run(e):
                def body(eng):
                    for waits, fn, sk, amt in items[e]:
                        for k, v in waits:
                            eng.wait_ge(sems[k], v)
                        if fn is not None:
                            fn(eng).then_inc(sems[sk], amt)
                return body
            block.sync(run("sp"))
            block.tensor(run("pe"))
            block.vector(run("dve"))
            block.scalar(run("act"))
            block.gpsimd(run("pool"))


class Arena:
    def __init__(self, t, nwords):
        self.t = t
        self.n = nwords
        self.off = 0

    def mark(self):
        return self.off

    def reset(self, m):
        self.off = m

    def alloc(self, shape, dt):
        n = 1
        for s in shape:
            n *= s
        words = n if dt in (F32, U32) else (n + 1) // 2
        a = self.off
        self.off += words
        assert self.off <= self.n, ("SBUF arena overflow", self.off, self.n, shape, a)
        v = self.t[:, a:a + words]
        if dt != F32:
            v = v.bitcast(dt)
            if dt == BF16 and n % 2:
                v = v[:, 0:n]
        if len(shape) == 2:
            return v.rearrange("p (a b) -> p a b", a=shape[0])
        if len(shape) == 3:
            return v.rearrange("p (a b c) -> p a b c", a=shape[0], b=shape[1])
        return v


WLIST = [("w_in_r", 8192), ("wv_r", 1024), ("wap_r", 1024), ("wcp_r", 1024), ("wout_r", 1024),
         ("wq_r", 2048), ("sk_r", 256), ("uv_r", 32768)]


def build_program(dbg=None):
    nc = bass.Bass("TRN2", target_bir_lowering=False)
    din = lambda n, s, dt=F32: nc.dram_tensor(n, s, dt, kind="ExternalInput").ap()
    x = din("x", [NTOK, D])
    wf = {n: din(n, [r, 1024]) for n, r in WLIST}
    wb = {n: nc.dram_tensor(n + "_b", [r, 1024], BF16, kind="Internal").ap() for n, r in WLIST}
    x2d = nc.dram_tensor("x2scr", [NTOK, D], F32, kind="Internal").ap()
    gmix = din("gmix", [D]); gffn = din("gffn", [D])
    qng = din("qng", [64]); kng = din("kng", [64])
    lams = din("lams", [4, 64])
    gsub = din("gsub", [128])
    cwd = din("cw", [128, 24])
    identb_d = din("identb", [128, 128], BF16); identf_d = din("identf", [128, 128])
    tri_d = din("tri", [128, 128], BF16); blk_d = din("blk64", [128, 128], BF16)
    iota_d = din("iota128", [128, 128], BF16); iota16_d = din("iota16", [128, 16])
    qpos_d = din("qpos", [8, 4, SEQ], BF16); kpos_d = din("kpos", [8, 4, SEQ], BF16)
    out = nc.dram_tensor("out", [NTOK, D], F32, kind="ExternalOutput").ap()
    dbg_out = {}
    if dbg:
        for n, shp in dbg.items():
            dbg_out[n] = nc.dram_tensor("dbg_" + n, list(shp), F32, kind="ExternalOutput").ap()

    with ExitStack() as st:
        S = Sched(nc, st)
        AW = 51 * 1024
        arena_t = st.enter_context(nc.sbuf_tensor("arena", [128, AW], F32))
        A = Arena(arena_t, AW)
        PS = [st.enter_context(nc.psum_tensor("ps%d" % i, [128, 512], F32))[:] for i in range(8)]
        BP = [S.buf("ps%d" % i) for i in range(8)]

        def MM(o, lhsT, rhs, start, stop, reads, writes):
            S.op("pe", lambda e: e.matmul(o, lhsT=lhsT, rhs=rhs, start=start, stop=stop), reads, writes)

        def TR(o, in_, ident, reads, writes):
            S.op("pe", lambda e: e.transpose(out=o, in_=in_, identity=ident), reads, writes)

        def ACT(o, in_, func, reads, writes, **kw):
            S.op("act", lambda e: e.activation(out=o, in_=in_, func=func, **kw), reads, writes)

        def TT(eng, o, in0, in1, op, reads, writes):
            S.op(eng, lambda e: e.tensor_tensor(out=o, in0=in0, in1=in1, op=op), reads, writes)

        def TS(eng, o, in0, s1, s2, op0, op1, reads, writes):
            if s2 is None:
                S.op(eng, lambda e: e.tensor_scalar(out=o, in0=in0, scalar1=s1, scalar2=None, op0=op0), reads, writes)
            else:
                S.op(eng, lambda e: e.tensor_scalar(out=o, in0=in0, scalar1=s1, scalar2=s2, op0=op0, op1=op1), reads, writes)

        def STT(eng, o, in0, sc, in1, op0, op1, reads, writes):
            S.op(eng, lambda e: e.scalar_tensor_tensor(out=o, in0=in0, scalar=sc, in1=in1, op0=op0, op1=op1), reads, writes)

        def CP(eng, o, in_, reads, writes):
            if eng == "act":
                ACT(o, in_, AF.Copy, reads, writes)
            else:
                S.op(eng, lambda e: e.tensor_copy(out=o, in_=in_), reads, writes)

        def MSET(eng, o, val, writes):
            S.op(eng, lambda e: e.memset(o, val), (), writes)

        def DMA(o, in_, reads, writes, sembuf=None):
            S.dma(lambda e: e.dma_start(out=o, in_=in_), reads, writes, sembuf)

        def RSTD(o, ss, n, rb, wbuf, tmp):
            ACT(tmp, ss, AF.Ln, rb, [wbuf], bias=EPS, scale=1.0 / n)
            ACT(o, tmp, AF.Exp, [wbuf], [wbuf], scale=-0.5)

        identb = A.alloc([128], BF16); identf = A.alloc([128], F32)
        tri = A.alloc([128], BF16); blk64 = A.alloc([128], BF16)
        iota = A.alloc([128], BF16); iota16 = A.alloc([16], F32)
        gq = A.alloc([1], F32); gk = A.alloc([1], F32)
        neglam = A.alloc([1], F32)
        gsubr = A.alloc([128], F32)
        cw = A.alloc([24], F32)
        grep = A.alloc([1024], F32)
        base_mark = A.mark()
        lamt = A.alloc([4, 64], F32); lamw = A.alloc([8], F32)
        Bc = S.buf("consts")
        Bgrep = S.buf("grep")
        DMA(identb, identb_d[:, :], [], [Bc]); DMA(identf, identf_d[:, :], [], [Bc])
        DMA(tri, tri_d[:, :], [], [Bc]); DMA(blk64, blk_d[:, :], [], [Bc])
        DMA(iota, iota_d[:, :], [], [Bc]); DMA(iota16, iota16_d[:, :], [], [Bc])
        DMA(cw, cwd[:, :], [], [Bc])
        DMA(gsubr, gsub.partition_broadcast(128), [], [Bc])
        for j in range(4):
            DMA(lamt[:, j, :], lams[j, :].partition_broadcast(128), [], [Bc])
        q1 = qng.rearrange("(d o) -> d o", o=1); k1 = kng.rearrange("(d o) -> d o", o=1)
        DMA(gq[0:64, :], q1, [], [Bc]); DMA(gq[64:128, :], q1, [], [Bc])
        DMA(gk[0:64, :], k1, [], [Bc]); DMA(gk[64:128, :], k1, [], [Bc])
        TS("dve", gq, gq, 0.125, None, ALU.mult, None, [Bc], [Bc])
        TS("dve", gsubr, gsubr, 1.0 - LAM_INIT, None, ALU.mult, None, [Bc], [Bc])
        TT("dve", lamt[:, 0, :], lamt[:, 0, :], lamt[:, 1, :], ALU.mult, [Bc], [Bc])
        TT("dve", lamt[:, 2, :], lamt[:, 2, :], lamt[:, 3, :], ALU.mult, [Bc], [Bc])
        S.op("dve", lambda e: e.tensor_reduce(out=lamw[:, 0:1], in_=lamt[:, 0, :], axis=AX.X, op=ALU.add), [Bc], [Bc])
        S.op("dve", lambda e: e.tensor_reduce(out=lamw[:, 1:2], in_=lamt[:, 2, :], axis=AX.X, op=ALU.add), [Bc], [Bc])
        ACT(lamw[:, 2:4], lamw[:, 0:2], AF.Exp, [Bc], [Bc])
        TT("dve", lamw[:, 4:5], lamw[:, 3:4], lamw[:, 2:3], ALU.subtract, [Bc], [Bc])
        TS("dve", neglam, lamw[:, 4:5], -LAM_INIT, None, ALU.add, None, [Bc], [Bc])
        S.barrier()
        A.reset(base_mark)

        NSL = 6
        cin = [A.alloc([2, 1024], F32) for _ in range(NSL)]
        cout = [A.alloc([2, 1024], BF16) for _ in range(NSL)]
        Bci = [S.buf("cin%d" % i) for i in range(NSL)]
        Bco = [S.buf("cout%d" % i) for i in range(NSL)]
        Bw = {n: S.buf("w_" + n) for n, _ in WLIST}
        jobs = []
        for n, rows in WLIST:
            if n == "uv_r":
                continue
            src = wf[n].rearrange("(s b p) c -> s p b c", b=2, p=128)
            dst = wb[n].rearrange("(s b p) c -> s p b c", b=2, p=128)
            for s_ in range(rows // 256):
                jobs.append((n, src[s_], dst[s_]))
        PD = 4
        for j in range(min(PD, len(jobs))):
            DMA(cin[j % NSL], jobs[j][1], [], [Bci[j % NSL]])
        for j, (n, src_, dst_) in enumerate(jobs):
            k = j % NSL
            if j + PD < len(jobs):
                DMA(cin[(j + PD) % NSL], jobs[j + PD][1], [], [Bci[(j + PD) % NSL]])
            CP(("act", "dve", "act", "dve", "pool")[j % 5], cout[k], cin[k], [Bci[k]], [Bco[k]])
            DMA(dst_, cout[k], [Bco[k]], [Bw[n]], sembuf=Bco[k])
        S.barrier()
        A.reset(base_mark)

        w_in_c = wb["w_in_r"].rearrange("(c p) n -> c p n", p=128)
        wap_c = wb["wap_r"].rearrange("(c p) n -> c p n", p=128)
        wcp_c = wb["wcp_r"].rearrange("(c p) n -> c p n", p=128)
        hT = A.alloc([8, SEQ], BF16); BhT = S.buf("hT")
        vtok = A.alloc([16 * 8, 129], BF16); Bv = S.buf("vtok")
        vtok4 = vtok.rearrange("p (t h) e -> p t h e", h=8)
        mixedT = vtok.rearrange("p a e -> p (a e)")[:, 0:8 * SEQ].rearrange("p (c t) -> p c t", c=8)
        attnT = A.alloc([8, SEQ], BF16); BaT = S.buf("attnT")
        convT = A.alloc([8, SEQ], BF16); BcT = S.buf("convT")
        mix_mark = A.mark()
        Bx2d = S.buf("x2d")

        def proj(psb, Bps, wt, Bwt, rhs_of_kc, Brhs):
            for kc in range(8):
                MM(psb, wt[:, kc, :], rhs_of_kc(kc), kc == 0, kc == 7, [Bwt, Brhs], [Bps])

        bg = {"done": 0}
        BG_N = 32768 // 128
        uv_src = wf["uv_r"].rearrange("(s p) c -> s p c", p=128)
        uv_dst = wb["uv_r"].rearrange("(s p) c -> s p c", p=128)

        def bg_step(bin_, bout_, Bbi, Bbo):
            j = bg["done"]
            if j >= BG_N:
                return
            k = j % 2
            if j == 0:
                S.dma(lambda e: e.dma_start(out=bin_[0], in_=uv_src[0]), [], [Bbi[0]], eng="pool")
            if j + 1 < BG_N:
                kn = (j + 1) % 2
                S.dma(lambda e: e.dma_start(out=bin_[kn], in_=uv_src[j + 1]), [], [Bbi[kn]], eng="pool")
            CP("dve", bout_[k], bin_[k], [Bbi[k]], [Bbo[k]])
            S.dma(lambda e: e.dma_start(out=uv_dst[j], in_=bout_[k]), [Bbo[k]], [Bw["uv_r"]], Bbo[k], eng="pool")
            bg["done"] = j + 1

        for sq_ in range(2):
            tok0 = sq_ * SEQ
            A.reset(mix_mark)
            xin = [A.alloc([1024], F32) for _ in range(2)]; Bx = [S.buf("xin%d" % i) for i in range(2)]
            hb = [A.alloc([1024], BF16) for _ in range(2)]; Bhb = [S.buf("hb%d" % i) for i in range(2)]
            junk = A.alloc([1024], BF16); Bj = S.buf("junk")
            st1 = [A.alloc([4], F32) for _ in range(2)]; Bs1 = [S.buf("st%d" % i) for i in range(2)]
            wbig = A.alloc([8, 1024], BF16); Bwbig = S.buf("wbig")
            if sq_ == 0:
                DMA(grep, gmix.partition_broadcast(128), [], [Bgrep])
            DMA(wbig.rearrange("p a b -> p (a b)"), wb["wv_r"].rearrange("(p a) c -> p (a c)", p=128), [Bw["wv_r"]], [Bwbig])
            for tt in range(16):
                k = tt % 2
                DMA(xin[k], x[tok0 + tt * 128: tok0 + (tt + 1) * 128, :], [], [Bx[k]])
                MSET("pool", st1[k][:, 0:1], 0.0, [Bs1[k]])
                ACT(junk, xin[k], AF.Square, [Bx[k]], [Bj, Bs1[k]], accum_out=st1[k][:, 0:1])
                RSTD(st1[k][:, 2:3], st1[k][:, 0:1], 1024.0, [Bs1[k]], Bs1[k], st1[k][:, 1:2])
                STT("dve", hb[k], xin[k], st1[k][:, 2:3], grep, ALU.mult, ALU.mult, [Bx[k], Bs1[k], Bgrep], [Bhb[k]])
                pb = PS[k].bitcast(BF16)
                for kc in range(8):
                    TR(pb[:, kc * 128:(kc + 1) * 128], hb[k][:, kc * 128:(kc + 1) * 128], identb, [Bhb[k], Bc], [BP[k]])
                CP("act" if k else "dve", hT[:, :, tt * 128:(tt + 1) * 128],
                   pb[:, 0:1024].rearrange("p (c t) -> p c t", c=8), [BP[k]], [BhT])
            MSET("pool", vtok[:, :, 128:129], 1.0, [Bv])
            for tt in range(16):
                for half in range(2):
                    b_ = 2 + (tt * 2 + half) % 4
                    for kc in range(8):
                        MM(PS[b_], hT[:, kc, tt * 128:(tt + 1) * 128], wbig[:, kc, half * 512:(half + 1) * 512],
                           kc == 0, kc == 7, [BhT, Bwbig], [BP[b_]])
                    CP("act" if half else "dve", vtok4[:, tt, 4 * half:4 * half + 4, 0:128],
                       PS[b_].rearrange("p (h e) -> p h e", h=4), [BP[b_]], [Bv])
            S.barrier()
            A.reset(mix_mark)
            QA = [A.alloc([SEQ], BF16) for _ in range(2)]; QB = [A.alloc([SEQ], BF16) for _ in range(2)]
            KA = [A.alloc([SEQ], BF16) for _ in range(2)]; KB = [A.alloc([SEQ], BF16) for _ in range(2)]
            BQ = [S.buf("Q%d" % i) for i in range(2)]; BK = [S.buf("K%d" % i) for i in range(2)]
            Bpos = [S.buf("pos%d" % i) for i in range(2)]
            wqk = [A.alloc([8, 128], BF16) for _ in range(2)]; Bwqk = [S.buf("wqk%d" % i) for i in range(2)]
            qf = [A.alloc([512], F32) for _ in range(2)]; Bqf = [S.buf("qf%d" % i) for i in range(2)]
            sqb = [A.alloc([512], BF16) for _ in range(2)]; Bsq = [S.buf("sq%d" % i) for i in range(2)]
            rsb = [A.alloc([512], F32) for _ in range(2)]; Brs = [S.buf("rs%d" % i) for i in range(2)]
            lnb = rsb
            NET = 4
            et = [A.alloc([512], BF16) for _ in range(NET)]; Bet = [S.buf("et%d" % i) for i in range(NET)]
            sm = [A.alloc([8], F32) for _ in range(2)]; Bsm = [S.buf("sm%d" % i) for i in range(2)]
            a0 = [A.alloc([128], F32) for _ in range(2)]; at_ = [A.alloc([128], F32) for _ in range(2)]
            an = [A.alloc([128], BF16) for _ in range(2)]
            Ba0 = [S.buf("a0%d" % i) for i in range(2)]
            junk2 = A.alloc([128], BF16); Bj2 = S.buf("junk2")
            gcnt = {"g": 0}
            SBK = (3, 4, 7)
            bg_in = [A.alloc([1024], F32) for _ in range(2)]; bg_out = [A.alloc([1024], BF16) for _ in range(2)]
            Bbg_i = [S.buf("bgi%d_%d" % (sq_, i)) for i in range(2)]; Bbg_o = [S.buf("bgo%d_%d" % (sq_, i)) for i in range(2)]
            if sq_ == 1 and bg["done"] < BG_N and bg["done"] > 0:
                jn = bg["done"]
                S.dma(lambda e: e.dma_start(out=bg_in[jn % 2], in_=uv_src[jn]), [], [Bbg_i[jn % 2]], eng="pool")

            def proj_steps(h):
                par = h % 2
                steps = []

                def s_pos():
                    DMA(QA[par][64:68, :], qpos_d[h], [], [Bpos[par]]); DMA(KA[par][64:68, :], kpos_d[h], [], [Bpos[par]])
                    DMA(QB[par][64:68, :], qpos_d[h], [], [Bpos[par]]); DMA(KB[par][64:68, :], kpos_d[h], [], [Bpos[par]])
                steps.append(s_pos)

                def s_w(which):
                    def f():
                        DMA(wqk[which].rearrange("p a b -> p (a b)"), w_in_c[which * 8 + h], [Bw["w_in_r"]], [Bwqk[which]])
                    return f

                def s_groupA(which, g):
                    def f():
                        for p in range(2):
                            for kc in range(8):
                                MM(PS[p][0:64, :], wqk[which][:, kc, p * 64:(p + 1) * 64], hT[:, kc, g * 512:(g + 1) * 512],
                                   kc == 0, kc == 7, [Bwqk[which], BhT], [BP[p]])
                            CP("act", qf[p][0:64, :], PS[p][0:64, :], [BP[p]], [Bqf[p]])
                            ACT(sqb[p][0:64, :], PS[p][0:64, :], AF.Square, [BP[p]], [Bsq[p]])
                    return f

                def s_groupB(which, g):
                    def f():
                        TA, TB, Bt, gv = (QA[par], QB[par], BQ[par], gq) if which == 0 else (KA[par], KB[par], BK[par], gk)
                        cs = slice(g * 512, (g + 1) * 512)
                        for p in range(2):
                            MM(PS[2][0:64, :], blk64[0:64, 0:64], sqb[p][0:64, :], True, True, [Bc, Bsq[p]], [BP[2]])
                            ACT(lnb[p][0:64, :], PS[2][0:64, :], AF.Ln, [BP[2]], [Brs[p]], bias=EPS, scale=1.0 / 64)
                            ACT(rsb[p][0:64, :], lnb[p][0:64, :], AF.Exp, [Brs[p]], [Brs[p]], scale=-0.5)
                            STT("dve", (TA, TB)[p][0:64, cs], qf[p][0:64, :], gv[0:64, 0:1], rsb[p][0:64, :], ALU.mult, ALU.mult,
                                [Bqf[p], Brs[p], Bc], [Bt])
                    return f
                for which in range(2):
                    steps.append(s_w(which))
                for which in range(2):
                    for g in range(4):
                        steps.append(s_groupA(which, g))
                        steps.append(s_groupB(which, g))
                return steps

            def attn_steps(h):
                par = h % 2
                qa, qb, ka, kb_ = QA[par], QB[par], KA[par], KB[par]
                groups = []
                for i in range(16):
                    for p in range(2):
                        for g0 in range(0, i + 1, 4):
                            groups.append((i, p, list(range(g0, min(g0 + 4, i + 1)))))

                def S_(gi):
                    def f():
                        i, p, kbs = groups[gi]
                        qs = slice(i * 128, (i + 1) * 128)
                        sb_ = SBK[gi % 3]
                        ek = gi % NET
                        for j, kb in enumerate(kbs):
                            ks = slice(kb * 128, (kb + 1) * 128)
                            o = PS[sb_][:, j * 128:(j + 1) * 128]
                            if p == 0:
                                MM(o, ka[0:68, ks], qa[0:68, qs], True, True, [BK[par], BQ[par], Bpos[par]], [BP[sb_]])
                            else:
                                MM(o, kb_[0:68, ks], qb[0:68, qs], True, True, [BK[par], BQ[par], Bpos[par]], [BP[sb_]])
                        n = len(kbs) * 128
                        ACT(et[ek][:, 0:n], PS[sb_][:, 0:n], AF.Exp, [BP[sb_]], [Bet[ek]])
                        if kbs[-1] == i:
                            j = len(kbs) - 1
                            TT("dve", et[ek][:, j * 128:(j + 1) * 128], et[ek][:, j * 128:(j + 1) * 128], tri, ALU.mult,
                               [Bet[ek], Bc], [Bet[ek]])
                    return f

                def A_(gi):
                    def f():
                        i, p, kbs = groups[gi]
                        ek = gi % NET
                        ub = 5 + i % 2
                        U = PS[ub][:, p * 256:p * 256 + 129]
                        for j, kb in enumerate(kbs):
                            MM(U, et[ek][:, j * 128:(j + 1) * 128], vtok4[:, kb, h, :], kb == 0, kb == i, [Bet[ek], Bv], [BP[ub]])
                        if p == 1 and kbs[-1] == i:
                            k = i % 2
                            U0 = PS[ub][:, 0:129]; U1 = PS[ub][:, 256:385]
                            S.op("dve", lambda e: e.reciprocal(out=sm[k][:, 0:1], in_=U0[:, 128:129]), [BP[ub]], [Bsm[k]])
                            S.op("dve", lambda e: e.reciprocal(out=sm[k][:, 1:2], in_=U1[:, 128:129]), [BP[ub]], [Bsm[k]])
                            TT("dve", sm[k][:, 2:3], sm[k][:, 1:2], neglam, ALU.mult, [Bsm[k], Bc], [Bsm[k]])
                            TS("dve", a0[k], U0[:, 0:128], sm[k][:, 0:1], None, ALU.mult, None, [BP[ub], Bsm[k]], [Ba0[k]])
                            STT("dve", at_[k], U1[:, 0:128], sm[k][:, 2:3], a0[k], ALU.mult, ALU.add, [BP[ub], Bsm[k], Ba0[k]], [Ba0[k]])
                            MSET("dve", sm[k][:, 3:4], 0.0, [Bsm[k]])
                            ACT(junk2, at_[k], AF.Square, [Ba0[k]], [Bj2, Bsm[k]], accum_out=sm[k][:, 3:4])
                            RSTD(sm[k][:, 5:6], sm[k][:, 3:4], 128.0, [Bsm[k]], Bsm[k], sm[k][:, 4:5])
                            STT("dve", an[k], at_[k], sm[k][:, 5:6], gsubr, ALU.mult, ALU.mult, [Ba0[k], Bsm[k], Bc], [Ba0[k]])
                    return f

                def T_(i):
                    def f():
                        k = i % 2
                        pT = PS[2].bitcast(BF16)[:, (i % 4) * 128:(i % 4 + 1) * 128]
                        TR(pT, an[k], identb, [Ba0[k], Bc], [BP[2]])
                        CP("act", attnT[:, h, i * 128:(i + 1) * 128], pT, [BP[2]], [BaT])
                    return f
                steps = []
                ng = len(groups)
                LA = 2
                for gi in range(min(LA, ng)):
                    steps.append(S_(gi))
                pend = []
                for gi in range(ng):
                    if gi + LA < ng:
                        steps.append(S_(gi + LA))
                    steps.append(A_(gi))
                    pend = [(i_, c_ - 1) for (i_, c_) in pend]
                    for (i_, c_) in pend:
                        if c_ <= 0:
                            steps.append(T_(i_))
                    pend = [(i_, c_) for (i_, c_) in pend if c_ > 0]
                    i, p, kbs = groups[gi]
                    if p == 1 and kbs[-1] == i:
                        pend.append((i, 3))
                for (i_, c_) in pend:
                    steps.append(T_(i_))
                return steps

            for f in proj_steps(0):
                f()
            for h in range(8):
                ast = attn_steps(h)
                pst = proj_steps(h + 1) if h + 1 < 8 else []
                na, npj = len(ast), len(pst)
                done = 0
                for j, f in enumerate(ast):
                    f()
                    if j % 5 == 2:
                        bg_step(bg_in, bg_out, Bbg_i, Bbg_o)
                    tgt = (npj * (j + 1)) // na
                    while done < tgt:
                        pst[done]()
                        done += 1
            S.barrier()
            A.reset(mix_mark)
            wc3 = [[A.alloc([8, 128], BF16) for _ in range(3)] for _ in range(2)]
            Bwc3 = [S.buf("wc3%d" % i) for i in range(2)]
            ybuf = A.alloc([SEQ + 2], F32); By = S.buf("ybuf")
            ccs = [A.alloc([512], F32) for _ in range(2)]; zb = [A.alloc([512], F32) for _ in range(2)]
            Bcs = [S.buf("ccs%d" % i) for i in range(2)]
            MSET("pool", ybuf[:, 0:2], 0.0, [By])
            gi = 0
            for c in range(8):
                ws = c % 2
                for j3, cc in enumerate((24 + c, 32 + c, 40 + c)):
                    DMA(wc3[ws][j3].rearrange("p a b -> p (a b)"), w_in_c[cc], [Bw["w_in_r"]], [Bwc3[ws]])
                for g in range(4):
                    k = gi % 2
                    gi += 1
                    b0 = 3 * k
                    rhs = lambda kc, g=g: hT[:, kc, g * 512:(g + 1) * 512]
                    proj(PS[b0], BP[b0], wc3[ws][1], Bwc3[ws], rhs, BhT)
                    proj(PS[b0 + 1], BP[b0 + 1], wc3[ws][2], Bwc3[ws], rhs, BhT)
                    proj(PS[b0 + 2], BP[b0 + 2], wc3[ws][0], Bwc3[ws], rhs, BhT)
                    CP("act", ccs[k], PS[b0], [BP[b0]], [Bcs[k]])
                    c0 = g * 512
                    TT("dve", ybuf[:, 2 + c0:2 + c0 + 512], ccs[k], PS[b0 + 1], ALU.mult, [Bcs[k], BP[b0 + 1]], [By])
                    TS("dve", zb[k], ybuf[:, 2 + c0:2 + c0 + 512], cw[:, c * 3 + 2:c * 3 + 3], None, ALU.mult, None, [By, Bc], [Bcs[k]])
                    STT("dve", zb[k], ybuf[:, 1 + c0:1 + c0 + 512], cw[:, c * 3 + 1:c * 3 + 2], zb[k], ALU.mult, ALU.add, [By, Bc, Bcs[k]], [Bcs[k]])
                    STT("dve", zb[k], ybuf[:, c0:c0 + 512], cw[:, c * 3:c * 3 + 1], zb[k], ALU.mult, ALU.add, [By, Bc, Bcs[k]], [Bcs[k]])
                    TT("dve", convT[:, c, c0:c0 + 512], zb[k], PS[b0 + 2], ALU.mult, [Bcs[k], BP[b0 + 2]], [BcT])
            S.barrier()
            A.reset(mix_mark)
            w4 = [[A.alloc([8, 128], BF16) for _ in range(4)] for _ in range(2)]
            Bw4 = [S.buf("w4%d" % i) for i in range(2)]
            sg = [[A.alloc([512], F32) for _ in range(4)] for _ in range(2)]
            Bsg = [S.buf("sg%d" % i) for i in range(2)]
            gi = 0
            for c in range(8):
                ws = c % 2
                srcs = (wap_c[c], wcp_c[c], w_in_c[48 + c], w_in_c[56 + c])
                deps = (Bw["wap_r"], Bw["wcp_r"], Bw["w_in_r"], Bw["w_in_r"])
                for j4 in range(4):
                    DMA(w4[ws][j4].rearrange("p a b -> p (a b)"), srcs[j4], [deps[j4]], [Bw4[ws]])
                for g in range(4):
                    k = gi % 2
                    gi += 1
                    b0 = 4 * k
                    gsl = slice(g * 512, (g + 1) * 512)
                    proj(PS[b0], BP[b0], w4[ws][0], Bw4[ws], lambda kc: attnT[:, kc, gsl], BaT)
                    proj(PS[b0 + 1], BP[b0 + 1], w4[ws][1], Bw4[ws], lambda kc: convT[:, kc, gsl], BcT)
                    proj(PS[b0 + 2], BP[b0 + 2], w4[ws][2], Bw4[ws], lambda kc: hT[:, kc, gsl], BhT)
                    proj(PS[b0 + 3], BP[b0 + 3], w4[ws][3], Bw4[ws], lambda kc: hT[:, kc, gsl], BhT)
                    ACT(sg[k][0], PS[b0 + 2], AF.Sigmoid, [BP[b0 + 2]], [Bsg[k]])
                    ACT(sg[k][1], PS[b0 + 3], AF.Sigmoid, [BP[b0 + 3]], [Bsg[k]])
                    TT("dve", sg[k][2], sg[k][0], PS[b0], ALU.mult, [Bsg[k], BP[b0]], [Bsg[k]])
                    TT("dve", sg[k][3], sg[k][1], PS[b0 + 1], ALU.mult, [Bsg[k], BP[b0 + 1]], [Bsg[k]])
                    TT("pool", mixedT[:, c, gsl], sg[k][2], sg[k][3], ALU.add, [Bsg[k]], [Bv])
            S.barrier()
            A.reset(mix_mark)
            xin = [A.alloc([1024], F32) for _ in range(2)]; Bx = [S.buf("xin%d" % i) for i in range(2)]
            x2t = [A.alloc([1024], F32) for _ in range(2)]; Bx2 = [S.buf("x2t%d" % i) for i in range(2)]
            wbig = A.alloc([8, 1024], BF16); Bwbig = S.buf("wbig")
            DMA(wbig.rearrange("p a b -> p (a b)"), wb["wout_r"].rearrange("(p a) c -> p (a c)", p=128), [Bw["wout_r"]], [Bwbig])
            for tt in range(16):
                k = tt % 2
                DMA(xin[k], x[tok0 + tt * 128: tok0 + (tt + 1) * 128, :], [], [Bx[k]])
                for half in range(2):
                    b_ = (tt * 2 + half) % 4
                    hs = slice(half * 512, (half + 1) * 512)
                    for kc in range(8):
                        MM(PS[b_], mixedT[:, kc, tt * 128:(tt + 1) * 128], wbig[:, kc, hs], kc == 0, kc == 7, [Bv, Bwbig], [BP[b_]])
                    TT("dve", x2t[k][:, hs], xin[k][:, hs], PS[b_], ALU.add, [Bx[k], BP[b_]], [Bx2[k]])
                DMA(x2d[tok0 + tt * 128: tok0 + (tt + 1) * 128, :], x2t[k], [Bx2[k]], [Bx2d], sembuf=Bx2[k])
            S.barrier()

        A.reset(base_mark)
        DMA(grep, gffn.partition_broadcast(128), [], [Bgrep])
        NST = PT // 128
        NTILE = NTOK // PT
        wq_c = wb["wq_r"].rearrange("(c p) n -> c p n", p=128)
        uv_c = wb["uv_r"].rearrange("(i p two) n -> i p (two n)", p=128, two=2)
        skt = A.alloc([16, 128], BF16); Bsk = S.buf("sk")
        DMA(skt.rearrange("p a b -> p (a b)"), wb["sk_r"].rearrange("(p a) c -> p (a c)", p=128), [Bw["sk_r"]], [Bsk])
        GTh = [A.alloc([PT, 64], BF16) for _ in range(2)]; BGT = [S.buf("GT%d" % i) for i in range(2)]
        x2k = [A.alloc([NST, 1024], F32) for _ in range(2)]; Bx2k = [S.buf("x2k%d" % i) for i in range(2)]
        h2T = [A.alloc([8, PT], BF16) for _ in range(2)]; Bh2T = [S.buf("h2T%d" % i) for i in range(2)]
        qT = A.alloc([16, PT], BF16); BqT = S.buf("qT")
        hb1 = A.alloc([1024], BF16); Bhb1 = S.buf("hbp")
        st1 = [A.alloc([4], F32) for _ in range(2)]; Bs1 = [S.buf("stp%d" % i) for i in range(2)]
        NWQ = 4
        wqs = [A.alloc([8, 128], BF16) for _ in range(NWQ)]; Bwqs = [S.buf("wqs%d" % i) for i in range(NWQ)]
        s_sb = A.alloc([16, 128], F32); s_w = A.alloc([16, 128], F32); Bs = S.buf("s_sb"); Bsw = S.buf("s_w")
        sv = A.alloc([16, 16], F32); si = A.alloc([16, 16], U32); sif = A.alloc([16, 16], F32); Bsv = S.buf("sv")
        cand = A.alloc([8 * 16, 16], F32); candw = A.alloc([8 * 16, 16], F32); Bcand = S.buf("cand"); Bcw = S.buf("candw")
        tv = A.alloc([8, 16], F32); posu = A.alloc([8, 16], U32); r0u = posu; r1u = A.alloc([8, 16], U32)
        r0f = A.alloc([8, 16], F32); r1f = A.alloc([8, 16], F32); Btv = S.buf("tv")
        IJg = A.alloc([3, 128], F32); BIJ = S.buf("IJg")
        tvz = A.alloc([8, 2], F32)
        ITs = A.alloc([3, PT], F32); BIT = S.buf("ITs")
        JTb = A.alloc([PT], BF16); ITb = A.alloc([PT], BF16)
        HT = 16
        AfT = [A.alloc([HT, 64], BF16) for _ in range(2)]; BfT = [A.alloc([HT, 128], BF16) for _ in range(2)]
        BAf = [S.buf("Af%d" % i) for i in range(2)]; BBf = [S.buf("Bf%d" % i) for i in range(2)]
        NUV = 7
        uvt = [A.alloc([2048], BF16) for _ in range(NUV)]
        uv = [[t_[:, 0:1024], t_[:, 1024:2048]] for t_ in uvt]
        Buv = [S.buf("uv%d" % i) for i in range(NUV)]
        glb = [A.alloc([PT], BF16) for _ in range(4)]; wTb = [A.alloc([PT], BF16) for _ in range(4)]
        Bgl = [S.buf("gl%d" % i) for i in range(4)]
        BPA = [S.buf("pa%d" % i) for i in range(3)]
        Bout = S.buf("out")
        sv4 = sv.rearrange("p (h two) r -> p h two r", two=2)
        sif4 = sif.rearrange("p (h two) r -> p h two r", two=2)
        cand4 = cand.rearrange("p (h a) b -> p h a b", h=8)
        candw4 = candw.rearrange("p (h a) b -> p h a b", h=8)
        candf = cand.rearrange("p (h a) b -> p h (a b)", h=8)
        candwf = candw.rearrange("p (h a) b -> p h (a b)", h=8)
        IJg4 = IJg.rearrange("p w (h r) -> p w h r", h=8)
        PDMA = lambda o, in_, reads, writes, sembuf=None: S.dma(lambda e: e.dma_start(out=o, in_=in_), reads, writes, sembuf, eng="pool")
        cnt = {"wq": 0, "uv": 0, "gl": 0, "g": 0}

        def topk_steps(T):
            steps = []
            pb_ = T % 2
            t0 = T * PT
            H2 = h2T[pb_]; X2 = x2k[pb_]

            def s_load(a):
                def f():
                    k = a % 2
                    PDMA(X2[:, a, :], x2d[t0 + a * 128:t0 + (a + 1) * 128, :], [Bx2d], [Bx2k[pb_]])
                    MSET("pool", st1[k][:, 0:1], 0.0, [Bs1[k]])
                    ACT(hb1, X2[:, a, :], AF.Square, [Bx2k[pb_]], [Bhb1, Bs1[k]], accum_out=st1[k][:, 0:1])
                    RSTD(st1[k][:, 2:3], st1[k][:, 0:1], 1024.0, [Bs1[k]], Bs1[k], st1[k][:, 1:2])
                    STT("dve", hb1, X2[:, a, :], st1[k][:, 2:3], grep, ALU.mult, ALU.mult, [Bx2k[pb_], Bs1[k], Bgrep], [Bhb1])
                return f

            def s_loadB(a):
                def f():
                    k = a % 2
                    pb = PS[4 + k].bitcast(BF16)
                    for kc in range(8):
                        TR(pb[:, kc * 128:(kc + 1) * 128], hb1[:, kc * 128:(kc + 1) * 128], identb, [Bhb1, Bc], [BP[4 + k]])
                    CP("act", H2[:, :, a * 128:(a + 1) * 128], pb[:, 0:1024].rearrange("p (c t) -> p c t", c=8), [BP[4 + k]], [Bh2T[pb_]])
                return f

            def s_wq(c):
                def f():
                    ws = (cnt["wq"] + c) % NWQ
                    PDMA(wqs[ws].rearrange("p a b -> p (a b)"), wq_c[c], [Bw["wq_r"]], [Bwqs[ws]])
                return f

            def s_q(c):
                def f():
                    ws = (cnt["wq"] + c) % NWQ
                    b_ = 4 + c % 2
                    for kc in range(8):
                        MM(PS[b_][:, 0:PT], wqs[ws][:, kc, :], H2[:, kc, :], kc == 0, kc == 7, [Bwqs[ws], Bh2T[pb_]], [BP[b_]])
                    CP("act", qT[:, c, :], PS[b_][:, 0:PT], [BP[b_]], [BqT])
                    if c == 15:
                        cnt["wq"] += 16
                return f

            def s_scores(a, q4):
                def f():
                    asl = slice(a * 128, (a + 1) * 128)
                    b_ = 4 + q4 % 2
                    for j in range(4):
                        hp = q4 * 4 + j
                        MM(PS[b_][:, j * 128:(j + 1) * 128], qT[:, hp, asl], skt[:, hp, :], True, True, [BqT, Bsk], [BP[b_]])
                    CP("act", s_sb[:, q4 * 4:q4 * 4 + 4, :], PS[b_].rearrange("p (a b) -> p a b", a=4), [BP[b_]], [Bs])
                return f

            def s_top(hp):
                def f():
                    S.op("dve", lambda e: e.max(out=sv[:, hp, 0:8], in_=s_sb[:, hp, :]), [Bs], [Bsv])
                    S.op("dve", lambda e: e.max_index(out=si[:, hp, 0:8], in_max=sv[:, hp, 0:8], in_values=s_sb[:, hp, :]), [Bs, Bsv], [Bsv])
                    S.op("dve", lambda e: e.match_replace(out=s_w[:, hp, :], in_to_replace=sv[:, hp, 0:8], in_values=s_sb[:, hp, :], imm_value=-1e30), [Bs, Bsv], [Bsw])
                    S.op("dve", lambda e: e.max(out=sv[:, hp, 8:16], in_=s_w[:, hp, :]), [Bsw], [Bsv])
                    S.op("dve", lambda e: e.max_index(out=si[:, hp, 8:16], in_max=sv[:, hp, 8:16], in_values=s_w[:, hp, :]), [Bsw, Bsv], [Bsv])
                return f

            def s_cand():
                CP("dve", sif, si, [Bsv], [Bsv])
                TT("dve", cand4, sv4[:, :, 0, :].unsqueeze(3).to_broadcast([128, 8, 16, 16]),
                   sv4[:, :, 1, :].unsqueeze(2).to_broadcast([128, 8, 16, 16]), ALU.add, [Bsv], [Bcand])

            def s_top2(h):
                def f():
                    S.op("dve", lambda e: e.max(out=tv[:, h, 0:8], in_=candf[:, h, :]), [Bcand], [Btv])
                    S.op("dve", lambda e: e.max_index(out=posu[:, h, 0:8], in_max=tv[:, h, 0:8], in_values=candf[:, h, :]), [Bcand, Btv], [Btv])
                    S.op("dve", lambda e: e.match_replace(out=candwf[:, h, :], in_to_replace=tv[:, h, 0:8], in_values=candf[:, h, :], imm_value=-1e30), [Bcand, Btv], [Bcw])
                    S.op("dve", lambda e: e.max(out=tv[:, h, 8:16], in_=candwf[:, h, :]), [Bcw], [Btv])
                    S.op("dve", lambda e: e.max_index(out=posu[:, h, 8:16], in_max=tv[:, h, 8:16], in_values=candwf[:, h, :]), [Bcw, Btv], [Btv])
                return f

            def s_idx(a):
                def f():
                    asl = slice(a * 128, (a + 1) * 128)
                    S.op("dve", lambda e: e.tensor_single_scalar(out=r1u, in_=posu, scalar=15, op=ALU.bitwise_and), [Btv], [Btv])
                    S.op("dve", lambda e: e.tensor_single_scalar(out=r0u, in_=posu, scalar=4, op=ALU.logical_shift_right), [Btv], [Btv])
                    CP("dve", r0f, r0u, [Btv], [Btv]); CP("dve", r1f, r1u, [Btv], [Btv])
                    for w_, rf in ((0, r0f), (1, r1f)):
                        TT("dve", candw4, iota16.unsqueeze(1).unsqueeze(1).to_broadcast([128, 8, 16, 16]),
                           rf.unsqueeze(3).to_broadcast([128, 8, 16, 16]), ALU.is_equal, [Btv, Bc, Bcand], [Bcw])
                        TT("dve", candw4, candw4, sif4[:, :, w_, :].unsqueeze(2).to_broadcast([128, 8, 16, 16]), ALU.mult, [Bcw, Bsv], [Bcw])
                        S.op("dve", lambda e, w_=w_: e.tensor_reduce(out=IJg4[:, w_, :, :], in_=candw4, axis=AX.X, op=ALU.add), [Bcw], [BIJ])
                    TT("dve", r0f, tv, tv[:, :, 0:1].to_broadcast([128, 8, 16]), ALU.subtract, [Btv], [Btv])
                    ACT(r0f, r0f, AF.Exp, [Btv], [Btv])
                    S.op("dve", lambda e: e.tensor_reduce(out=tvz[:, :, 0], in_=r0f, axis=AX.X, op=ALU.add), [Btv], [Btv])
                    S.op("dve", lambda e: e.reciprocal(out=tvz[:, :, 1], in_=tvz[:, :, 0]), [Btv], [Btv])
                    TT("dve", IJg4[:, 2, :, :], r0f, tvz[:, :, 1:2].to_broadcast([128, 8, 16]), ALU.mult, [Btv], [BIJ])
                return f

            def s_idxB(a):
                def f():
                    asl = slice(a * 128, (a + 1) * 128)
                    for w_ in range(3):
                        TR(PS[4][:, w_ * 128:(w_ + 1) * 128], IJg[:, w_, :], identf, [BIJ, Bc], [BP[4]])
                    CP("act", ITs[:, :, asl], PS[4][:, 0:384].rearrange("p (w t) -> p w t", w=3), [BP[4]], [BIT])
                return f

            def s_final():
                CP("act", JTb, ITs[:, 1, :], [BIT], [BIT])
                CP("act", ITb, ITs[:, 0, :], [BIT], [BIT])

            wqsteps = [s_wq(c) for c in range(16)]
            steps.append(s_load(0)); steps.append(wqsteps[0]); steps.append(wqsteps[1]); steps.append(s_loadB(0))
            steps.append(s_load(1)); steps.append(wqsteps[2]); steps.append(s_loadB(1))
            for c in range(16):
                if c + 3 < 16:
                    steps.append(wqsteps[c + 3])
                steps.append(s_q(c))
            for q4 in range(4):
                steps.append(s_scores(0, q4))
            for hp in range(16):
                steps.append(s_top(hp))
            steps.append(s_cand)
            for h in range(8):
                steps.append(s_top2(h))
            steps.append(s_idx(0))
            for q4 in range(4):
                steps.append(s_scores(1, q4))
            steps.append(s_idxB(0))
            for hp in range(16):
                steps.append(s_top(hp))
            steps.append(s_cand)
            for h in range(8):
                steps.append(s_top2(h))
            steps.append(s_idx(1))
            steps.append(s_idxB(1))
            steps.append(s_final)
            return steps

        def cons_steps(T, half):
            steps = []
            G_ = GTh[half]

            def prod(hb_):
                def f():
                    k = hb_ % 2
                    ts = slice(hb_ * HT, (hb_ + 1) * HT)
                    TT("dve", BfT[k], iota.unsqueeze(1).to_broadcast([128, HT, 128]), JTb[:, ts].unsqueeze(2).to_broadcast([128, HT, 128]),
                       ALU.is_equal, [BIT, Bc], [BBf[k]])
                    TT("dve", AfT[k], iota[:, half * 64:(half + 1) * 64].unsqueeze(1).to_broadcast([128, HT, 64]),
                       ITb[:, ts].unsqueeze(2).to_broadcast([128, HT, 64]), ALU.is_equal, [BIT, Bc], [BAf[k]])
                    TT("dve", AfT[k], AfT[k], ITs[:, 2, ts].unsqueeze(2).to_broadcast([128, HT, 64]), ALU.mult, [BAf[k], BIT], [BAf[k]])
                return f

            def mm(hb_):
                def f():
                    k = hb_ % 2
                    for q8 in range(HT // 8):
                        b_ = 4 + q8 % 2
                        for j in range(8):
                            tl = q8 * 8 + j
                            MM(PS[b_][:, j * 64:(j + 1) * 64], BfT[k][:, tl, :], AfT[k][:, tl, :], True, True, [BAf[k], BBf[k]], [BP[b_]])
                        tg = hb_ * HT + q8 * 8
                        CP("act", G_[:, tg:tg + 8, :], PS[b_].rearrange("p (t i) -> p t i", t=8), [BP[b_]], [BGT[half]])
                return f
            nb = PT // HT
            steps.append(prod(0))
            for hb_ in range(nb):
                if hb_ + 1 < nb:
                    p_, m_ = prod(hb_ + 1), mm(hb_)
                    steps.append(lambda p_=p_, m_=m_: (p_(), m_()))
                else:
                    steps.append(mm(hb_))
            return steps

        def uv_load(g):
            if g >= NTILE * 128:
                return
            us = g % NUV
            i = g % 128
            DMA(uvt[us], uv_c[i], [Bw["uv_r"]], [Buv[us]])

        PF = NUV - 4
        NSLOT = 4

        def U_(g):
            T, i = divmod(g, 128)
            pb_ = T % 2
            uv_load(g + PF)
            us = g % NUV
            uT = uv[us][0].rearrange("p (a b) -> p a b", a=8)
            k = g % NSLOT
            pa = PS[6 + g % 2][:, 0:PT]
            bpa = BPA[g % 2]
            for kc in range(8):
                MM(pa, uT[:, kc, :], h2T[pb_][:, kc, :], kc == 0, kc == 7, [Buv[us], Bh2T[pb_]], [bpa])
            ACT(glb[k], pa, AF.Gelu, [bpa], [Bgl[k]])
            TT("pool", wTb[k], glb[k], GTh[i // 64][:, :, i % 64], ALU.mult, [Bgl[k], BGT[i // 64]], [Bgl[k]])

        def V_(g):
            T, i = divmod(g, 128)
            us = g % NUV
            k = g % NSLOT
            for a in range(NST):
                for hf in range(2):
                    MM(PS[a * 2 + hf], wTb[k][:, a * 128:(a + 1) * 128], uv[us][1][:, hf * 512:(hf + 1) * 512],
                       i == 0, i == 127, [Bgl[k], Buv[us]], [BP[a * 2 + hf]])

        def finish(T):
            pb_ = T % 2
            t0 = T * PT
            for a in range(NST):
                for hf in range(2):
                    hs = slice(hf * 512, (hf + 1) * 512)
                    TT("dve", x2k[pb_][:, a, hs], x2k[pb_][:, a, hs], PS[a * 2 + hf], ALU.add, [Bx2k[pb_], BP[a * 2 + hf]], [Bx2k[pb_]])
                PDMA(out[t0 + a * 128:t0 + (a + 1) * 128, :], x2k[pb_][:, a, :], [Bx2k[pb_]], [Bout], sembuf=Bx2k[pb_])

        for f in topk_steps(0) + cons_steps(0, 0):
            f()
        for g in range(PF):
            uv_load(g)
        NG = NTILE * 128
        LA = 3
        for g in range(LA):
            U_(g)
        for T in range(NTILE):
            c1 = cons_steps(T, 1)
            tk = topk_steps(T + 1) if T + 1 < NTILE else []
            c0 = cons_steps(T + 1, 0) if T + 1 < NTILE else []
            split = len(tk) - 29 if tk else 0
            st_a = c1 + tk[:split]
            st_b = tk[split:] + c0
            for (lo, steps) in ((0, st_a), (64, st_b)):
                ns = len(steps)
                done = 0
                for j in range(64):
                    g = T * 128 + lo + j
                    if g + LA < NG:
                        U_(g + LA)
                    V_(g)
                    tgt = min(ns, (ns * (j + 1)) // 56)
                    while done < tgt:
                        steps[done]()
                        done += 1
            finish(T)
        S.barrier()
        S.replay()
    return nc


def _consts():
    c = {}
    c["identb"] = np.eye(128).astype(NPBF)
    c["identf"] = np.eye(128, dtype=np.float32)
    kk = np.arange(128)
    c["tri"] = (kk[None, :] >= kk[:, None]).astype(NPBF)
    c["blk64"] = ((kk[:, None] // 64) == (kk[None, :] // 64)).astype(NPBF)
    c["iota128"] = np.tile(np.arange(128, dtype=np.float32), (128, 1)).astype(NPBF)
    c["iota16"] = np.tile(np.arange(16, dtype=np.float32), (128, 1))
    pos = np.arange(SEQ)
    qq = (pos % 128).astype(np.float32); qb = (pos // 128).astype(np.float32)
    qpos = np.zeros((8, 4, SEQ), np.float32); kpos = np.zeros((8, 4, SEQ), np.float32)
    for h in range(8):
        sl = 2.0 ** (-(h + 1))
        qpos[h, 0] = -sl * qq; qpos[h, 1] = -sl * 128.0 * qb; qpos[h, 2] = 1.0; qpos[h, 3] = 1.0
        kpos[h, 0] = 1.0; kpos[h, 1] = 1.0; kpos[h, 2] = sl * qq; kpos[h, 3] = sl * 128.0 * qb
    c["qpos"] = qpos.astype(NPBF); c["kpos"] = kpos.astype(NPBF)
    return c


def _layouts(w_in, w_attn_proj, w_conv_proj, w_out, peer_w_query, peer_sub_keys, peer_u, peer_v):
    f = lambda a: np.ascontiguousarray(a, dtype=np.float32)
    L = {}
    colchunk = lambda w, ncc: f(w.reshape(8, 128, ncc, 128).transpose(2, 1, 0, 3)).reshape(ncc * 128, 1024)
    rowmajor = lambda w: f(w.reshape(8, 128, w.shape[1]).transpose(1, 0, 2)).reshape(-1, 1024)
    L["w_in_r"] = colchunk(w_in, 64)
    L["wv_r"] = rowmajor(w_in[:, 2048:3072])
    L["wap_r"] = colchunk(w_attn_proj, 8)
    L["wcp_r"] = colchunk(w_conv_proj, 8)
    L["wout_r"] = rowmajor(w_out)
    L["wq_r"] = colchunk(peer_w_query, 16)
    L["sk_r"] = f(peer_sub_keys.reshape(16, 128, 128).transpose(2, 0, 1)).reshape(256, 1024)
    u_r = peer_u.reshape(128, 128, 8, 128).transpose(0, 3, 2, 1).reshape(128, 128, 1, 1024)
    v_r = peer_v.reshape(128, 128, 1, 1024)
    L["uv_r"] = f(np.concatenate([u_r, v_r], axis=2)).reshape(32768, 1024)
    return L


_NC_CACHE = {}


def kernel(x, norm_mix_g, w_in, q_norm_g, k_norm_g, lambda_q1, lambda_k1, lambda_q2, lambda_k2,
           subln_g, w_attn_proj, conv_w, w_conv_proj, w_out, norm_ffn_g,
           peer_w_query, peer_sub_keys, peer_u, peer_v, _dbg=None):
    A_ = lambda a: np.asarray(a)
    x = A_(x).astype(np.float32, copy=False)
    L = _layouts(A_(w_in)[0], A_(w_attn_proj)[0], A_(w_conv_proj)[0], A_(w_out)[0], A_(peer_w_query)[0],
                 A_(peer_sub_keys)[0], A_(peer_u)[0], A_(peer_v)[0])
    C = _consts()
    shared = dict(L)
    shared.update(C)
    shared["gmix"] = np.ascontiguousarray(A_(norm_mix_g)[0], np.float32)
    shared["gffn"] = np.ascontiguousarray(A_(norm_ffn_g)[0], np.float32)
    shared["qng"] = np.ascontiguousarray(A_(q_norm_g)[0], np.float32)
    shared["kng"] = np.ascontiguousarray(A_(k_norm_g)[0], np.float32)
    shared["lams"] = np.ascontiguousarray(np.stack([A_(lambda_q1)[0], A_(lambda_k1)[0], A_(lambda_q2)[0], A_(lambda_k2)[0]]), np.float32)
    shared["gsub"] = np.ascontiguousarray(A_(subln_g)[0], np.float32)
    shared["cw"] = np.ascontiguousarray(A_(conv_w)[0].reshape(3, 8, 128).transpose(2, 1, 0).reshape(128, 24), np.float32)
    key = repr(sorted(_dbg.items())) if _dbg else ""
    if key not in _NC_CACHE:
        _NC_CACHE[key] = build_program(_dbg)
    nc = _NC_CACHE[key]
    xs = x.reshape(NCORES, NTOK, D)
    in_maps = []
    for c in range(NCORES):
        m = dict(shared)
        m["x"] = np.ascontiguousarray(xs[c])
        in_maps.append(m)
    res = run_bass_kernel_spmd(nc, in_maps, core_ids=list(range(NCORES)))
    outs = np.stack([np.asarray(r["out"]) for r in res.results]).reshape(16, SEQ, D).astype(np.float32)
    if _dbg:
        return outs, res.results
    return outs
```

```python
import numpy as np
import ml_dtypes
from contextlib import ExitStack
import concourse.bass as bass
import concourse.mybir as mybir
from concourse.bass_utils import run_bass_kernel_spmd

F32 = mybir.dt.float32
BF16 = mybir.dt.bfloat16
U32 = mybir.dt.uint32
ALU = mybir.AluOpType
AF = mybir.ActivationFunctionType
AX = mybir.AxisListType
NPBF = ml_dtypes.bfloat16

NCORES = 8
SEQ = 2048
D = 1024
NTOK = 2 * SEQ
EPS = 1e-6
LAM_INIT = 0.8 - 0.6 * 1.0
PT = 256


class Buf:
    __slots__ = ("name", "w", "r", "dsem", "dcount")

    def __init__(self, name):
        self.name = name
        self.w = None
        self.r = []
        self.dsem = None
        self.dcount = 0


class Sched:
    ENG = ("pe", "act", "dve", "pool", "sp")

    def __init__(self, nc, stack):
        self.nc = nc
        self.stack = stack
        self.items = {e: [] for e in self.ENG}
        self.count = {e: 0 for e in self.ENG}
        self.known = {e: {} for e in self.ENG}
        self.sems = {}
        self.nsem = 0
        self.bufs = []
        for e in ("pe", "act", "dve", "pool"):
            self._sem("eng_" + e)

    def buf(self, name):
        b = Buf(name)
        self.bufs.append(b)
        return b

    def _sem(self, key):
        if key not in self.sems:
            self.sems[key] = self.stack.enter_context(self.nc.semaphore("s%d" % self.nsem))
            self.nsem += 1
        return key

    def _collect(self, eng, reads, writes):
        need = {}

        def add(ev):
            if ev is None:
                return
            k, v = ev
            if eng == "pe" and k == "eng_pe":
                return
            if need.get(k, 0) < v:
                need[k] = v

        for b in reads:
            add(b.w)
        for b in writes:
            add(b.w)
            for ev in b.r:
                add(ev)
        kn = self.known[eng]
        waits = []
        for k, v in need.items():
            if kn.get(k, 0) >= v:
                continue
            kn[k] = v
            waits.append((k, v))
        return waits

    def _post(self, ev, reads, writes):
        for b in reads:
            if len(b.r) > 6:
                m = {}
                for k, v in b.r:
                    if m.get(k, 0) < v:
                        m[k] = v
                b.r = list(m.items())
            b.r.append(ev)
        for b in writes:
            b.w = ev
            b.r = []

    def op(self, eng, fn, reads=(), writes=()):
        waits = self._collect(eng, reads, writes)
        self.count[eng] += 1
        ev = ("eng_" + eng, self.count[eng])
        self.items[eng].append((waits, fn, ev[0], 1))
        self._post(ev, reads, writes)
        return ev

    def dma(self, fn, reads=(), writes=(), sembuf=None, eng="sp"):
        waits = self._collect(eng, reads, writes)
        if sembuf is None:
            sembuf = writes[0] if writes else reads[0]
        if sembuf.dsem is None:
            sembuf.dsem = self._sem("dma%d" % self.nsem)
        sembuf.dcount += 1
        ev = (sembuf.dsem, 16 * sembuf.dcount)
        self.items[eng].append((waits, fn, ev[0], 16))
        self._post(ev, reads, writes)
        return ev

    def wait_all(self, eng, bufs):
        waits = self._collect(eng, (), bufs)
        if waits:
            self.items[eng].append((waits, None, None, 0))

    def barrier(self):
        for e in self.ENG:
            self.wait_all(e, self.bufs)

    def replay(self):
        nc = self.nc
        sems = self.sems
        items = self.items
        with nc.Block() as block:
            def run(e):
                def body(eng):
                    for waits, fn, sk, amt in items[e]:
                        for k, v in waits:
                            eng.wait_ge(sems[k], v)
                        if fn is not None:
                            fn(eng).then_inc(sems[sk], amt)
                return body
            block.sync(run("sp"))
            block.tensor(run("pe"))
            block.vector(run("dve"))
            block.scalar(run("act"))
            block.gpsimd(run("pool"))


class Arena:
    def __init__(self, t, nwords):
        self.t = t
        self.n = nwords
        self.off = 0

    def mark(self):
        return self.off

    def reset(self, m):
        self.off = m

    def alloc(self, shape, dt):
        n = 1
        for s in shape:
            n *= s
        words = n if dt in (F32, U32) else (n + 1) // 2
        a = self.off
        self.off += words
        assert self.off <= self.n, ("SBUF arena overflow", self.off, self.n, shape, a)
        v = self.t[:, a:a + words]
        if dt != F32:
            v = v.bitcast(dt)
            if dt == BF16 and n % 2:
                v = v[:, 0:n]
        if len(shape) == 2:
            return v.rearrange("p (a b) -> p a b", a=shape[0])
        if len(shape) == 3:
            return v.rearrange("p (a b c) -> p a b c", a=shape[0], b=shape[1])
        return v


WLIST = [("w_in_r", 8192), ("wv_r", 1024), ("wap_r", 1024), ("wcp_r", 1024), ("wout_r", 1024),
         ("wq_r", 2048), ("sk_r", 256), ("uv_r", 32768)]


def build_program(dbg=None):
    nc = bass.Bass("TRN2", target_bir_lowering=False)
    din = lambda n, s, dt=F32: nc.dram_tensor(n, s, dt, kind="ExternalInput").ap()
    x = din("x", [NTOK, D])
    wf = {n: din(n, [r, 1024]) for n, r in WLIST}
    wb = {n: nc.dram_tensor(n + "_b", [r, 1024], BF16, kind="Internal").ap() for n, r in WLIST}
    x2d = nc.dram_tensor("x2scr", [NTOK, D], F32, kind="Internal").ap()
    gmix = din("gmix", [D]); gffn = din("gffn", [D])
    qng = din("qng", [64]); kng = din("kng", [64])
    lams = din("lams", [4, 64])
    gsub = din("gsub", [128])
    cwd = din("cw", [128, 24])
    identb_d = din("identb", [128, 128], BF16); identf_d = din("identf", [128, 128])
    tri_d = din("tri", [128, 128], BF16); blk_d = din("blk64", [128, 128], BF16)
    iota_d = din("iota128", [128, 128], BF16); iota16_d = din("iota16", [128, 16])
    qpos_d = din("qpos", [8, 4, SEQ], BF16); kpos_d = din("kpos", [8, 4, SEQ], BF16)
    out = nc.dram_tensor("out", [NTOK, D], F32, kind="ExternalOutput").ap()
    dbg_out = {}
    if dbg:
        for n, shp in dbg.items():
            dbg_out[n] = nc.dram_tensor("dbg_" + n, list(shp), F32, kind="ExternalOutput").ap()

    with ExitStack() as st:
        S = Sched(nc, st)
        AW = 51 * 1024
        arena_t = st.enter_context(nc.sbuf_tensor("arena", [128, AW], F32))
        A = Arena(arena_t, AW)
        PS = [st.enter_context(nc.psum_tensor("ps%d" % i, [128, 512], F32))[:] for i in range(8)]
        BP = [S.buf("ps%d" % i) for i in range(8)]

        def MM(o, lhsT, rhs, start, stop, reads, writes):
            S.op("pe", lambda e: e.matmul(o, lhsT=lhsT, rhs=rhs, start=start, stop=stop), reads, writes)

        def TR(o, in_, ident, reads, writes):
            S.op("pe", lambda e: e.transpose(out=o, in_=in_, identity=ident), reads, writes)

        def ACT(o, in_, func, reads, writes, **kw):
            S.op("act", lambda e: e.activation(out=o, in_=in_, func=func, **kw), reads, writes)

        def TT(eng, o, in0, in1, op, reads, writes):
            S.op(eng, lambda e: e.tensor_tensor(out=o, in0=in0, in1=in1, op=op), reads, writes)

        def TS(eng, o, in0, s1, s2, op0, op1, reads, writes):
            if s2 is None:
                S.op(eng, lambda e: e.tensor_scalar(out=o, in0=in0, scalar1=s1, scalar2=None, op0=op0), reads, writes)
            else:
                S.op(eng, lambda e: e.tensor_scalar(out=o, in0=in0, scalar1=s1, scalar2=s2, op0=op0, op1=op1), reads, writes)

        def STT(eng, o, in0, sc, in1, op0, op1, reads, writes):
            S.op(eng, lambda e: e.scalar_tensor_tensor(out=o, in0=in0, scalar=sc, in1=in1, op0=op0, op1=op1), reads, writes)

        def CP(eng, o, in_, reads, writes):
            if eng == "act":
                ACT(o, in_, AF.Copy, reads, writes)
            else:
                S.op(eng, lambda e: e.tensor_copy(out=o, in_=in_), reads, writes)

        def MSET(eng, o, val, writes):
            S.op(eng, lambda e: e.memset(o, val), (), writes)

        def DMA(o, in_, reads, writes, sembuf=None):
            S.dma(lambda e: e.dma_start(out=o, in_=in_), reads, writes, sembuf)

        def RSTD(o, ss, n, rb, wbuf, tmp):
            ACT(tmp, ss, AF.Ln, rb, [wbuf], bias=EPS, scale=1.0 / n)
            ACT(o, tmp, AF.Exp, [wbuf], [wbuf], scale=-0.5)

        identb = A.alloc([128], BF16); identf = A.alloc([128], F32)
        tri = A.alloc([128], BF16); blk64 = A.alloc([128], BF16)
        iota = A.alloc([128], BF16); iota16 = A.alloc([16], F32)
        gq = A.alloc([1], F32); gk = A.alloc([1], F32)
        neglam = A.alloc([1], F32)
        gsubr = A.alloc([128], F32)
        cw = A.alloc([24], F32)
        grep = A.alloc([1024], F32)
        lamt = A.alloc([4, 64], F32); lamw = A.alloc([8], F32)
        Bc = S.buf("consts")
        Bgrep = S.buf("grep")
        DMA(identb, identb_d[:, :], [], [Bc]); DMA(identf, identf_d[:, :], [], [Bc])
        DMA(tri, tri_d[:, :], [], [Bc]); DMA(blk64, blk_d[:, :], [], [Bc])
        DMA(iota, iota_d[:, :], [], [Bc]); DMA(iota16, iota16_d[:, :], [], [Bc])
        DMA(cw, cwd[:, :], [], [Bc])
        DMA(gsubr, gsub.partition_broadcast(128), [], [Bc])
        for j in range(4):
            DMA(lamt[:, j, :], lams[j, :].partition_broadcast(128), [], [Bc])
        q1 = qng.rearrange("(d o) -> d o", o=1); k1 = kng.rearrange("(d o) -> d o", o=1)
        DMA(gq[0:64, :], q1, [], [Bc]); DMA(gq[64:128, :], q1, [], [Bc])
        DMA(gk[0:64, :], k1, [], [Bc]); DMA(gk[64:128, :], k1, [], [Bc])
        TS("dve", gq, gq, 0.125, None, ALU.mult, None, [Bc], [Bc])
        TS("dve", gsubr, gsubr, 1.0 - LAM_INIT, None, ALU.mult, None, [Bc], [Bc])
        TT("dve", lamt[:, 0, :], lamt[:, 0, :], lamt[:, 1, :], ALU.mult, [Bc], [Bc])
        TT("dve", lamt[:, 2, :], lamt[:, 2, :], lamt[:, 3, :], ALU.mult, [Bc], [Bc])
        S.op("dve", lambda e: e.tensor_reduce(out=lamw[:, 0:1], in_=lamt[:, 0, :], axis=AX.X, op=ALU.add), [Bc], [Bc])
        S.op("dve", lambda e: e.tensor_reduce(out=lamw[:, 1:2], in_=lamt[:, 2, :], axis=AX.X, op=ALU.add), [Bc], [Bc])
        ACT(lamw[:, 2:4], lamw[:, 0:2], AF.Exp, [Bc], [Bc])
        TT("dve", lamw[:, 4:5], lamw[:, 3:4], lamw[:, 2:3], ALU.subtract, [Bc], [Bc])
        TS("dve", neglam, lamw[:, 4:5], -LAM_INIT, None, ALU.add, None, [Bc], [Bc])
        base_mark = A.mark()

        NSL = 6
        cin = [A.alloc([2, 1024], F32) for _ in range(NSL)]
        cout = [A.alloc([2, 1024], BF16) for _ in range(NSL)]
        Bci = [S.buf("cin%d" % i) for i in range(NSL)]
        Bco = [S.buf("cout%d" % i) for i in range(NSL)]
        Bw = {n: S.buf("w_" + n) for n, _ in WLIST}
        jobs = []
        for n, rows in WLIST:
            if n == "uv_r":
                continue
            src = wf[n].rearrange("(s b p) c -> s p b c", b=2, p=128)
            dst = wb[n].rearrange("(s b p) c -> s p b c", b=2, p=128)
            for s_ in range(rows // 256):
                jobs.append((n, src[s_], dst[s_]))
        PD = 4
        for j in range(min(PD, len(jobs))):
            DMA(cin[j % NSL], jobs[j][1], [], [Bci[j % NSL]])
        for j, (n, src_, dst_) in enumerate(jobs):
            k = j % NSL
            if j + PD < len(jobs):
                DMA(cin[(j + PD) % NSL], jobs[j + PD][1], [], [Bci[(j + PD) % NSL]])
            CP(("act", "dve", "act", "dve", "pool")[j % 5], cout[k], cin[k], [Bci[k]], [Bco[k]])
            DMA(dst_, cout[k], [Bco[k]], [Bw[n]], sembuf=Bco[k])
        S.barrier()
        A.reset(base_mark)

        w_in_c = wb["w_in_r"].rearrange("(c p) n -> c p n", p=128)
        wap_c = wb["wap_r"].rearrange("(c p) n -> c p n", p=128)
        wcp_c = wb["wcp_r"].rearrange("(c p) n -> c p n", p=128)
        hT = A.alloc([8, SEQ], BF16); BhT = S.buf("hT")
        vtok = A.alloc([16 * 8, 129], BF16); Bv = S.buf("vtok")
        vtok4 = vtok.rearrange("p (t h) e -> p t h e", h=8)
        mixedT = vtok.rearrange("p a e -> p (a e)")[:, 0:8 * SEQ].rearrange("p (c t) -> p c t", c=8)
        attnT = A.alloc([8, SEQ], BF16); BaT = S.buf("attnT")
        convT = A.alloc([8, SEQ], BF16); BcT = S.buf("convT")
        mix_mark = A.mark()
        Bx2d = S.buf("x2d")

        def proj(psb, Bps, wt, Bwt, rhs_of_kc, Brhs):
            for kc in range(8):
                MM(psb, wt[:, kc, :], rhs_of_kc(kc), kc == 0, kc == 7, [Bwt, Brhs], [Bps])

        bg = {"done": 0}
        BG_N = 32768 // 128
        uv_src = wf["uv_r"].rearrange("(s p) c -> s p c", p=128)
        uv_dst = wb["uv_r"].rearrange("(s p) c -> s p c", p=128)

        def bg_step(bin_, bout_, Bbi, Bbo):
            j = bg["done"]
            if j >= BG_N:
                return
            k = j % 2
            if j == 0:
                S.dma(lambda e: e.dma_start(out=bin_[0], in_=uv_src[0]), [], [Bbi[0]], eng="pool")
            if j + 1 < BG_N:
                kn = (j + 1) % 2
                S.dma(lambda e: e.dma_start(out=bin_[kn], in_=uv_src[j + 1]), [], [Bbi[kn]], eng="pool")
            CP("dve", bout_[k], bin_[k], [Bbi[k]], [Bbo[k]])
            S.dma(lambda e: e.dma_start(out=uv_dst[j], in_=bout_[k]), [Bbo[k]], [Bw["uv_r"]], Bbo[k], eng="pool")
            bg["done"] = j + 1

        for sq_ in range(2):
            tok0 = sq_ * SEQ
            A.reset(mix_mark)
            xin = [A.alloc([1024], F32) for _ in range(2)]; Bx = [S.buf("xin%d" % i) for i in range(2)]
            hb = [A.alloc([1024], BF16) for _ in range(2)]; Bhb = [S.buf("hb%d" % i) for i in range(2)]
            junk = A.alloc([1024], BF16); Bj = S.buf("junk")
            st1 = [A.alloc([4], F32) for _ in range(2)]; Bs1 = [S.buf("st%d" % i) for i in range(2)]
            wbig = A.alloc([8, 1024], BF16); Bwbig = S.buf("wbig")
            if sq_ == 0:
                DMA(grep, gmix.partition_broadcast(128), [], [Bgrep])
            DMA(wbig.rearrange("p a b -> p (a b)"), wb["wv_r"].rearrange("(p a) c -> p (a c)", p=128), [Bw["wv_r"]], [Bwbig])
            for tt in range(16):
                k = tt % 2
                DMA(xin[k], x[tok0 + tt * 128: tok0 + (tt + 1) * 128, :], [], [Bx[k]])
                MSET("pool", st1[k][:, 0:1], 0.0, [Bs1[k]])
                ACT(junk, xin[k], AF.Square, [Bx[k]], [Bj, Bs1[k]], accum_out=st1[k][:, 0:1])
                RSTD(st1[k][:, 2:3], st1[k][:, 0:1], 1024.0, [Bs1[k]], Bs1[k], st1[k][:, 1:2])
                STT("dve", hb[k], xin[k], st1[k][:, 2:3], grep, ALU.mult, ALU.mult, [Bx[k], Bs1[k], Bgrep], [Bhb[k]])
                pb = PS[k].bitcast(BF16)
                for kc in range(8):
                    TR(pb[:, kc * 128:(kc + 1) * 128], hb[k][:, kc * 128:(kc + 1) * 128], identb, [Bhb[k], Bc], [BP[k]])
                CP("act" if k else "dve", hT[:, :, tt * 128:(tt + 1) * 128],
                   pb[:, 0:1024].rearrange("p (c t) -> p c t", c=8), [BP[k]], [BhT])
            MSET("pool", vtok[:, :, 128:129], 1.0, [Bv])
            for tt in range(16):
                for half in range(2):
                    b_ = 2 + (tt * 2 + half) % 4
                    for kc in range(8):
                        MM(PS[b_], hT[:, kc, tt * 128:(tt + 1) * 128], wbig[:, kc, half * 512:(half + 1) * 512],
                           kc == 0, kc == 7, [BhT, Bwbig], [BP[b_]])
                    CP("act" if half else "dve", vtok4[:, tt, 4 * half:4 * half + 4, 0:128],
                       PS[b_].rearrange("p (h e) -> p h e", h=4), [BP[b_]], [Bv])
            S.barrier()
            A.reset(mix_mark)
            QA = [A.alloc([SEQ], BF16) for _ in range(2)]; QB = [A.alloc([SEQ], BF16) for _ in range(2)]
            KA = [A.alloc([SEQ], BF16) for _ in range(2)]; KB = [A.alloc([SEQ], BF16) for _ in range(2)]
            BQ = [S.buf("Q%d" % i) for i in range(2)]; BK = [S.buf("K%d" % i) for i in range(2)]
            Bpos = [S.buf("pos%d" % i) for i in range(2)]
            wqk = [A.alloc([8, 128], BF16) for _ in range(2)]; Bwqk = [S.buf("wqk%d" % i) for i in range(2)]
            qf = [A.alloc([512], F32) for _ in range(2)]; Bqf = [S.buf("qf%d" % i) for i in range(2)]
            sqb = [A.alloc([512], BF16) for _ in range(2)]; Bsq = [S.buf("sq%d" % i) for i in range(2)]
            rsb = [A.alloc([512], F32) for _ in range(2)]; Brs = [S.buf("rs%d" % i) for i in range(2)]
            lnb = rsb
            NET = 4
            et = [A.alloc([512], BF16) for _ in range(NET)]; Bet = [S.buf("et%d" % i) for i in range(NET)]
            sm = [A.alloc([8], F32) for _ in range(2)]; Bsm = [S.buf("sm%d" % i) for i in range(2)]
            a0 = [A.alloc([128], F32) for _ in range(2)]; at_ = [A.alloc([128], F32) for _ in range(2)]
            an = [A.alloc([128], BF16) for _ in range(2)]
            Ba0 = [S.buf("a0%d" % i) for i in range(2)]
            junk2 = A.alloc([128], BF16); Bj2 = S.buf("junk2")
            gcnt = {"g": 0}
            SBK = (3, 4, 7)
            bg_in = [A.alloc([1024], F32) for _ in range(2)]; bg_out = [A.alloc([1024], BF16) for _ in range(2)]
            Bbg_i = [S.buf("bgi%d_%d" % (sq_, i)) for i in range(2)]; Bbg_o = [S.buf("bgo%d_%d" % (sq_, i)) for i in range(2)]
            if sq_ == 1 and bg["done"] < BG_N and bg["done"] > 0:
                jn = bg["done"]
                S.dma(lambda e: e.dma_start(out=bg_in[jn % 2], in_=uv_src[jn]), [], [Bbg_i[jn % 2]], eng="pool")

            def proj_steps(h):
                par = h % 2
                steps = []

                def s_pos():
                    DMA(QA[par][64:68, :], qpos_d[h], [], [Bpos[par]]); DMA(KA[par][64:68, :], kpos_d[h], [], [Bpos[par]])
                    DMA(QB[par][64:68, :], qpos_d[h], [], [Bpos[par]]); DMA(KB[par][64:68, :], kpos_d[h], [], [Bpos[par]])
                steps.append(s_pos)

                def s_w(which):
                    def f():
                        DMA(wqk[which].rearrange("p a b -> p (a b)"), w_in_c[which * 8 + h], [Bw["w_in_r"]], [Bwqk[which]])
                    return f

                def s_groupA(which, g):
                    def f():
                        for p in range(2):
                            for kc in range(8):
                                MM(PS[p][0:64, :], wqk[which][:, kc, p * 64:(p + 1) * 64], hT[:, kc, g * 512:(g + 1) * 512],
                                   kc == 0, kc == 7, [Bwqk[which], BhT], [BP[p]])
                            CP("act", qf[p][0:64, :], PS[p][0:64, :], [BP[p]], [Bqf[p]])
                            ACT(sqb[p][0:64, :], PS[p][0:64, :], AF.Square, [BP[p]], [Bsq[p]])
                    return f

                def s_groupB(which, g):
                    def f():
                        TA, TB, Bt, gv = (QA[par], QB[par], BQ[par], gq) if which == 0 else (KA[par], KB[par], BK[par], gk)
                        cs = slice(g * 512, (g + 1) * 512)
                        for p in range(2):
                            MM(PS[2][0:64, :], blk64[0:64, 0:64], sqb[p][0:64, :], True, True, [Bc, Bsq[p]], [BP[2]])
                            ACT(lnb[p][0:64, :], PS[2][0:64, :], AF.Ln, [BP[2]], [Brs[p]], bias=EPS, scale=1.0 / 64)
                            ACT(rsb[p][0:64, :], lnb[p][0:64, :], AF.Exp, [Brs[p]], [Brs[p]], scale=-0.5)
                            STT("dve", (TA, TB)[p][0:64, cs], qf[p][0:64, :], gv[0:64, 0:1], rsb[p][0:64, :], ALU.mult, ALU.mult,
                                [Bqf[p], Brs[p], Bc], [Bt])
                    return f
                for which in range(2):
                    steps.append(s_w(which))
                for which in range(2):
                    for g in range(4):
                        steps.append(s_groupA(which, g))
                        steps.append(s_groupB(which, g))
                return steps

            def attn_steps(h):
                par = h % 2
                qa, qb, ka, kb_ = QA[par], QB[par], KA[par], KB[par]
                groups = []
                for i in range(16):
                    for p in range(2):
                        for g0 in range(0, i + 1, 4):
                            groups.append((i, p, list(range(g0, min(g0 + 4, i + 1)))))

                def S_(gi):
                    def f():
                        i, p, kbs = groups[gi]
                        qs = slice(i * 128, (i + 1) * 128)
                        sb_ = SBK[gi % 3]
                        ek = gi % NET
                        for j, kb in enumerate(kbs):
                            ks = slice(kb * 128, (kb + 1) * 128)
                            o = PS[sb_][:, j * 128:(j + 1) * 128]
                            if p == 0:
                                MM(o, ka[0:68, ks], qa[0:68, qs], True, True, [BK[par], BQ[par], Bpos[par]], [BP[sb_]])
                            else:
                                MM(o, kb_[0:68, ks], qb[0:68, qs], True, True, [BK[par], BQ[par], Bpos[par]], [BP[sb_]])
                        n = len(kbs) * 128
                        ACT(et[ek][:, 0:n], PS[sb_][:, 0:n], AF.Exp, [BP[sb_]], [Bet[ek]])
                        if kbs[-1] == i:
                            j = len(kbs) - 1
                            TT("dve", et[ek][:, j * 128:(j + 1) * 128], et[ek][:, j * 128:(j + 1) * 128], tri, ALU.mult,
                               [Bet[ek], Bc], [Bet[ek]])
                    return f

                def A_(gi):
                    def f():
                        i, p, kbs = groups[gi]
                        ek = gi % NET
                        ub = 5 + i % 2
                        U = PS[ub][:, p * 256:p * 256 + 129]
                        for j, kb in enumerate(kbs):
                            MM(U, et[ek][:, j * 128:(j + 1) * 128], vtok4[:, kb, h, :], kb == 0, kb == i, [Bet[ek], Bv], [BP[ub]])
                        if p == 1 and kbs[-1] == i:
                            k = i % 2
                            U0 = PS[ub][:, 0:129]; U1 = PS[ub][:, 256:385]
                            S.op("dve", lambda e: e.reciprocal(out=sm[k][:, 0:1], in_=U0[:, 128:129]), [BP[ub]], [Bsm[k]])
                            S.op("dve", lambda e: e.reciprocal(out=sm[k][:, 1:2], in_=U1[:, 128:129]), [BP[ub]], [Bsm[k]])
                            TT("dve", sm[k][:, 2:3], sm[k][:, 1:2], neglam, ALU.mult, [Bsm[k], Bc], [Bsm[k]])
                            TS("dve", a0[k], U0[:, 0:128], sm[k][:, 0:1], None, ALU.mult, None, [BP[ub], Bsm[k]], [Ba0[k]])
                            STT("dve", at_[k], U1[:, 0:128], sm[k][:, 2:3], a0[k], ALU.mult, ALU.add, [BP[ub], Bsm[k], Ba0[k]], [Ba0[k]])
                            MSET("dve", sm[k][:, 3:4], 0.0, [Bsm[k]])
                            ACT(junk2, at_[k], AF.Square, [Ba0[k]], [Bj2, Bsm[k]], accum_out=sm[k][:, 3:4])
                            RSTD(sm[k][:, 5:6], sm[k][:, 3:4], 128.0, [Bsm[k]], Bsm[k], sm[k][:, 4:5])
                            STT("dve", an[k], at_[k], sm[k][:, 5:6], gsubr, ALU.mult, ALU.mult, [Ba0[k], Bsm[k], Bc], [Ba0[k]])
                    return f

                def T_(i):
                    def f():
                        k = i % 2
                        pT = PS[2].bitcast(BF16)[:, (i % 4) * 128:(i % 4 + 1) * 128]
                        TR(pT, an[k], identb, [Ba0[k], Bc], [BP[2]])
                        CP("act", attnT[:, h, i * 128:(i + 1) * 128], pT, [BP[2]], [BaT])
                    return f
                steps = []
                ng = len(groups)
                LA = 2
                for gi in range(min(LA, ng)):
                    steps.append(S_(gi))
                pend = []
                for gi in range(ng):
                    if gi + LA < ng:
                        steps.append(S_(gi + LA))
                    steps.append(A_(gi))
                    pend = [(i_, c_ - 1) for (i_, c_) in pend]
                    for (i_, c_) in pend:
                        if c_ <= 0:
                            steps.append(T_(i_))
                    pend = [(i_, c_) for (i_, c_) in pend if c_ > 0]
                    i, p, kbs = groups[gi]
                    if p == 1 and kbs[-1] == i:
                        pend.append((i, 3))
                for (i_, c_) in pend:
                    steps.append(T_(i_))
                return steps

            for f in proj_steps(0):
                f()
            for h in range(8):
                ast = attn_steps(h)
                pst = proj_steps(h + 1) if h + 1 < 8 else []
                na, npj = len(ast), len(pst)
                done = 0
                for j, f in enumerate(ast):
                    f()
                    if j % 5 == 2:
                        bg_step(bg_in, bg_out, Bbg_i, Bbg_o)
                    tgt = (npj * (j + 1)) // na
                    while done < tgt:
                        pst[done]()
                        done += 1
            S.barrier()
            A.reset(mix_mark)
            wc3 = [[A.alloc([8, 128], BF16) for _ in range(3)] for _ in range(2)]
            Bwc3 = [S.buf("wc3%d" % i) for i in range(2)]
            ybuf = A.alloc([SEQ + 2], F32); By = S.buf("ybuf")
            ccs = [A.alloc([512], F32) for _ in range(2)]; zb = [A.alloc([512], F32) for _ in range(2)]
            Bcs = [S.buf("ccs%d" % i) for i in range(2)]
            MSET("pool", ybuf[:, 0:2], 0.0, [By])
            gi = 0
            for c in range(8):
                ws = c % 2
                for j3, cc in enumerate((24 + c, 32 + c, 40 + c)):
                    DMA(wc3[ws][j3].rearrange("p a b -> p (a b)"), w_in_c[cc], [Bw["w_in_r"]], [Bwc3[ws]])
                for g in range(4):
                    k = gi % 2
                    gi += 1
                    b0 = 3 * k
                    rhs = lambda kc, g=g: hT[:, kc, g * 512:(g + 1) * 512]
                    proj(PS[b0], BP[b0], wc3[ws][1], Bwc3[ws], rhs, BhT)
                    proj(PS[b0 + 1], BP[b0 + 1], wc3[ws][2], Bwc3[ws], rhs, BhT)
                    proj(PS[b0 + 2], BP[b0 + 2], wc3[ws][0], Bwc3[ws], rhs, BhT)
                    CP("act", ccs[k], PS[b0], [BP[b0]], [Bcs[k]])
                    c0 = g * 512
                    TT("dve", ybuf[:, 2 + c0:2 + c0 + 512], ccs[k], PS[b0 + 1], ALU.mult, [Bcs[k], BP[b0 + 1]], [By])
                    TS("dve", zb[k], ybuf[:, 2 + c0:2 + c0 + 512], cw[:, c * 3 + 2:c * 3 + 3], None, ALU.mult, None, [By, Bc], [Bcs[k]])
                    STT("dve", zb[k], ybuf[:, 1 + c0:1 + c0 + 512], cw[:, c * 3 + 1:c * 3 + 2], zb[k], ALU.mult, ALU.add, [By, Bc, Bcs[k]], [Bcs[k]])
                    STT("dve", zb[k], ybuf[:, c0:c0 + 512], cw[:, c * 3:c * 3 + 1], zb[k], ALU.mult, ALU.add, [By, Bc, Bcs[k]], [Bcs[k]])
                    TT("dve", convT[:, c, c0:c0 + 512], zb[k], PS[b0 + 2], ALU.mult, [Bcs[k], BP[b0 + 2]], [BcT])
            S.barrier()
            A.reset(mix_mark)
            w4 = [[A.alloc([8, 128], BF16) for _ in range(4)] for _ in range(2)]
            Bw4 = [S.buf("w4%d" % i) for i in range(2)]
            sg = [[A.alloc([512], F32) for _ in range(4)] for _ in range(2)]
            Bsg = [S.buf("sg%d" % i) for i in range(2)]
            gi = 0
            for c in range(8):
                ws = c % 2
                srcs = (wap_c[c], wcp_c[c], w_in_c[48 + c], w_in_c[56 + c])
                deps = (Bw["wap_r"], Bw["wcp_r"], Bw["w_in_r"], Bw["w_in_r"])
                for j4 in range(4):
                    DMA(w4[ws][j4].rearrange("p a b -> p (a b)"), srcs[j4], [deps[j4]], [Bw4[ws]])
                for g in range(4):
                    k = gi % 2
                    gi += 1
                    b0 = 4 * k
                    gsl = slice(g * 512, (g + 1) * 512)
                    proj(PS[b0], BP[b0], w4[ws][0], Bw4[ws], lambda kc: attnT[:, kc, gsl], BaT)
                    proj(PS[b0 + 1], BP[b0 + 1], w4[ws][1], Bw4[ws], lambda kc: convT[:, kc, gsl], BcT)
                    proj(PS[b0 + 2], BP[b0 + 2], w4[ws][2], Bw4[ws], lambda kc: hT[:, kc, gsl], BhT)
                    proj(PS[b0 + 3], BP[b0 + 3], w4[ws][3], Bw4[ws], lambda kc: hT[:, kc, gsl], BhT)
                    ACT(sg[k][0], PS[b0 + 2], AF.Sigmoid, [BP[b0 + 2]], [Bsg[k]])
                    ACT(sg[k][1], PS[b0 + 3], AF.Sigmoid, [BP[b0 + 3]], [Bsg[k]])
                    TT("dve", sg[k][2], sg[k][0], PS[b0], ALU.mult, [Bsg[k], BP[b0]], [Bsg[k]])
                    TT("dve", sg[k][3], sg[k][1], PS[b0 + 1], ALU.mult, [Bsg[k], BP[b0 + 1]], [Bsg[k]])
                    TT("pool", mixedT[:, c, gsl], sg[k][2], sg[k][3], ALU.add, [Bsg[k]], [Bv])
            S.barrier()
            A.reset(mix_mark)
            xin = [A.alloc([1024], F32) for _ in range(2)]; Bx = [S.buf("xin%d" % i) for i in range(2)]
            x2t = [A.alloc([1024], F32) for _ in range(2)]; Bx2 = [S.buf("x2t%d" % i) for i in range(2)]
            wbig = A.alloc([8, 1024], BF16); Bwbig = S.buf("wbig")
            DMA(wbig.rearrange("p a b -> p (a b)"), wb["wout_r"].rearrange("(p a) c -> p (a c)", p=128), [Bw["wout_r"]], [Bwbig])
            for tt in range(16):
                k = tt % 2
                DMA(xin[k], x[tok0 + tt * 128: tok0 + (tt + 1) * 128, :], [], [Bx[k]])
                for half in range(2):
                    b_ = (tt * 2 + half) % 4
                    hs = slice(half * 512, (half + 1) * 512)
                    for kc in range(8):
                        MM(PS[b_], mixedT[:, kc, tt * 128:(tt + 1) * 128], wbig[:, kc, hs], kc == 0, kc == 7, [Bv, Bwbig], [BP[b_]])
                    TT("dve", x2t[k][:, hs], xin[k][:, hs], PS[b_], ALU.add, [Bx[k], BP[b_]], [Bx2[k]])
                DMA(x2d[tok0 + tt * 128: tok0 + (tt + 1) * 128, :], x2t[k], [Bx2[k]], [Bx2d], sembuf=Bx2[k])
            S.barrier()

        A.reset(base_mark)
        DMA(grep, gffn.partition_broadcast(128), [], [Bgrep])
        NST = PT // 128
        NTILE = NTOK // PT
        wq_c = wb["wq_r"].rearrange("(c p) n -> c p n", p=128)
        uv_c = wb["uv_r"].rearrange("(i p two) n -> i p (two n)", p=128, two=2)
        skt = A.alloc([16, 128], BF16); Bsk = S.buf("sk")
        DMA(skt.rearrange("p a b -> p (a b)"), wb["sk_r"].rearrange("(p a) c -> p (a c)", p=128), [Bw["sk_r"]], [Bsk])
        GTh = [A.alloc([PT, 64], BF16) for _ in range(2)]; BGT = [S.buf("GT%d" % i) for i in range(2)]
        x2k = [A.alloc([NST, 1024], F32) for _ in range(2)]; Bx2k = [S.buf("x2k%d" % i) for i in range(2)]
        h2T = [A.alloc([8, PT], BF16) for _ in range(2)]; Bh2T = [S.buf("h2T%d" % i) for i in range(2)]
        qT = A.alloc([16, PT], BF16); BqT = S.buf("qT")
        hb1 = A.alloc([1024], BF16); Bhb1 = S.buf("hbp")
        st1 = [A.alloc([4], F32) for _ in range(2)]; Bs1 = [S.buf("stp%d" % i) for i in range(2)]
        NWQ = 4
        wqs = [A.alloc([8, 128], BF16) for _ in range(NWQ)]; Bwqs = [S.buf("wqs%d" % i) for i in range(NWQ)]
        s_sb = A.alloc([16, 128], F32); s_w = A.alloc([16, 128], F32); Bs = S.buf("s_sb"); Bsw = S.buf("s_w")
        sv = A.alloc([16, 16], F32); si = A.alloc([16, 16], U32); sif = A.alloc([16, 16], F32); Bsv = S.buf("sv")
        cand = A.alloc([8 * 16, 16], F32); candw = A.alloc([8 * 16, 16], F32); Bcand = S.buf("cand"); Bcw = S.buf("candw")
        tv = A.alloc([8, 16], F32); posu = A.alloc([8, 16], U32); r0u = posu; r1u = A.alloc([8, 16], U32)
        r0f = A.alloc([8, 16], F32); r1f = A.alloc([8, 16], F32); Btv = S.buf("tv")
        IJg = A.alloc([3, 128], F32); BIJ = S.buf("IJg")
        tvz = A.alloc([8, 2], F32)
        ITs = A.alloc([3, PT], F32); BIT = S.buf("ITs")
        JTb = A.alloc([PT], BF16); ITb = A.alloc([PT], BF16)
        HT = 16
        AfT = [A.alloc([HT, 64], BF16) for _ in range(2)]; BfT = [A.alloc([HT, 128], BF16) for _ in range(2)]
        BAf = [S.buf("Af%d" % i) for i in range(2)]; BBf = [S.buf("Bf%d" % i) for i in range(2)]
        NUV = 6
        uvt = [A.alloc([2048], BF16) for _ in range(NUV)]
        uv = [[t_[:, 0:1024], t_[:, 1024:2048]] for t_ in uvt]
        Buv = [S.buf("uv%d" % i) for i in range(NUV)]
        glb = [A.alloc([PT], BF16) for _ in range(4)]; wTb = [A.alloc([PT], BF16) for _ in range(4)]
        Bgl = [S.buf("gl%d" % i) for i in range(4)]
        BPA = [S.buf("pa%d" % i) for i in range(3)]
        Bout = S.buf("out")
        sv4 = sv.rearrange("p (h two) r -> p h two r", two=2)
        sif4 = sif.rearrange("p (h two) r -> p h two r", two=2)
        cand4 = cand.rearrange("p (h a) b -> p h a b", h=8)
        candw4 = candw.rearrange("p (h a) b -> p h a b", h=8)
        candf = cand.rearrange("p (h a) b -> p h (a b)", h=8)
        candwf = candw.rearrange("p (h a) b -> p h (a b)", h=8)
        IJg4 = IJg.rearrange("p w (h r) -> p w h r", h=8)
        PDMA = lambda o, in_, reads, writes, sembuf=None: S.dma(lambda e: e.dma_start(out=o, in_=in_), reads, writes, sembuf, eng="pool")
        cnt = {"wq": 0, "uv": 0, "gl": 0, "g": 0}

        def topk_steps(T):
            steps = []
            pb_ = T % 2
            t0 = T * PT
            H2 = h2T[pb_]; X2 = x2k[pb_]

            def s_load(a):
                def f():
                    k = a % 2
                    PDMA(X2[:, a, :], x2d[t0 + a * 128:t0 + (a + 1) * 128, :], [Bx2d], [Bx2k[pb_]])
                    MSET("pool", st1[k][:, 0:1], 0.0, [Bs1[k]])
                    ACT(hb1, X2[:, a, :], AF.Square, [Bx2k[pb_]], [Bhb1, Bs1[k]], accum_out=st1[k][:, 0:1])
                    RSTD(st1[k][:, 2:3], st1[k][:, 0:1], 1024.0, [Bs1[k]], Bs1[k], st1[k][:, 1:2])
                    STT("dve", hb1, X2[:, a, :], st1[k][:, 2:3], grep, ALU.mult, ALU.mult, [Bx2k[pb_], Bs1[k], Bgrep], [Bhb1])
                return f

            def s_loadB(a):
                def f():
                    k = a % 2
                    pb = PS[4 + k].bitcast(BF16)
                    for kc in range(8):
                        TR(pb[:, kc * 128:(kc + 1) * 128], hb1[:, kc * 128:(kc + 1) * 128], identb, [Bhb1, Bc], [BP[4 + k]])
                    CP("act", H2[:, :, a * 128:(a + 1) * 128], pb[:, 0:1024].rearrange("p (c t) -> p c t", c=8), [BP[4 + k]], [Bh2T[pb_]])
                return f

            def s_wq(c):
                def f():
                    ws = (cnt["wq"] + c) % NWQ
                    PDMA(wqs[ws].rearrange("p a b -> p (a b)"), wq_c[c], [Bw["wq_r"]], [Bwqs[ws]])
                return f

            def s_q(c):
                def f():
                    ws = (cnt["wq"] + c) % NWQ
                    b_ = 4 + c % 2
                    for kc in range(8):
                        MM(PS[b_][:, 0:PT], wqs[ws][:, kc, :], H2[:, kc, :], kc == 0, kc == 7, [Bwqs[ws], Bh2T[pb_]], [BP[b_]])
                    CP("act", qT[:, c, :], PS[b_][:, 0:PT], [BP[b_]], [BqT])
                    if c == 15:
                        cnt["wq"] += 16
                return f

            def s_scores(a, q4):
                def f():
                    asl = slice(a * 128, (a + 1) * 128)
                    b_ = 4 + q4 % 2
                    for j in range(4):
                        hp = q4 * 4 + j
                        MM(PS[b_][:, j * 128:(j + 1) * 128], qT[:, hp, asl], skt[:, hp, :], True, True, [BqT, Bsk], [BP[b_]])
                    CP("act", s_sb[:, q4 * 4:q4 * 4 + 4, :], PS[b_].rearrange("p (a b) -> p a b", a=4), [BP[b_]], [Bs])
                return f

            def s_top(hp):
                def f():
                    S.op("dve", lambda e: e.max(out=sv[:, hp, 0:8], in_=s_sb[:, hp, :]), [Bs], [Bsv])
                    S.op("dve", lambda e: e.max_index(out=si[:, hp, 0:8], in_max=sv[:, hp, 0:8], in_values=s_sb[:, hp, :]), [Bs, Bsv], [Bsv])
                    S.op("dve", lambda e: e.match_replace(out=s_w[:, hp, :], in_to_replace=sv[:, hp, 0:8], in_values=s_sb[:, hp, :], imm_value=-1e30), [Bs, Bsv], [Bsw])
                    S.op("dve", lambda e: e.max(out=sv[:, hp, 8:16], in_=s_w[:, hp, :]), [Bsw], [Bsv])
                    S.op("dve", lambda e: e.max_index(out=si[:, hp, 8:16], in_max=sv[:, hp, 8:16], in_values=s_w[:, hp, :]), [Bsw, Bsv], [Bsv])
                return f

            def s_cand():
                CP("dve", sif, si, [Bsv], [Bsv])
                TT("dve", cand4, sv4[:, :, 0, :].unsqueeze(3).to_broadcast([128, 8, 16, 16]),
                   sv4[:, :, 1, :].unsqueeze(2).to_broadcast([128, 8, 16, 16]), ALU.add, [Bsv], [Bcand])

            def s_top2(h):
                def f():
                    S.op("dve", lambda e: e.max(out=tv[:, h, 0:8], in_=candf[:, h, :]), [Bcand], [Btv])
                    S.op("dve", lambda e: e.max_index(out=posu[:, h, 0:8], in_max=tv[:, h, 0:8], in_values=candf[:, h, :]), [Bcand, Btv], [Btv])
                    S.op("dve", lambda e: e.match_replace(out=candwf[:, h, :], in_to_replace=tv[:, h, 0:8], in_values=candf[:, h, :], imm_value=-1e30), [Bcand, Btv], [Bcw])
                    S.op("dve", lambda e: e.max(out=tv[:, h, 8:16], in_=candwf[:, h, :]), [Bcw], [Btv])
                    S.op("dve", lambda e: e.max_index(out=posu[:, h, 8:16], in_max=tv[:, h, 8:16], in_values=candwf[:, h, :]), [Bcw, Btv], [Btv])
                return f

            def s_idx(a):
                def f():
                    asl = slice(a * 128, (a + 1) * 128)
                    S.op("dve", lambda e: e.tensor_single_scalar(out=r1u, in_=posu, scalar=15, op=ALU.bitwise_and), [Btv], [Btv])
                    S.op("dve", lambda e: e.tensor_single_scalar(out=r0u, in_=posu, scalar=4, op=ALU.logical_shift_right), [Btv], [Btv])
                    CP("dve", r0f, r0u, [Btv], [Btv]); CP("dve", r1f, r1u, [Btv], [Btv])
                    for w_, rf in ((0, r0f), (1, r1f)):
                        TT("dve", candw4, iota16.unsqueeze(1).unsqueeze(1).to_broadcast([128, 8, 16, 16]),
                           rf.unsqueeze(3).to_broadcast([128, 8, 16, 16]), ALU.is_equal, [Btv, Bc, Bcand], [Bcw])
                        TT("dve", candw4, candw4, sif4[:, :, w_, :].unsqueeze(2).to_broadcast([128, 8, 16, 16]), ALU.mult, [Bcw, Bsv], [Bcw])
                        S.op("dve", lambda e, w_=w_: e.tensor_reduce(out=IJg4[:, w_, :, :], in_=candw4, axis=AX.X, op=ALU.add), [Bcw], [BIJ])
                    TT("dve", r0f, tv, tv[:, :, 0:1].to_broadcast([128, 8, 16]), ALU.subtract, [Btv], [Btv])
                    ACT(r0f, r0f, AF.Exp, [Btv], [Btv])
                    S.op("dve", lambda e: e.tensor_reduce(out=tvz[:, :, 0], in_=r0f, axis=AX.X, op=ALU.add), [Btv], [Btv])
                    S.op("dve", lambda e: e.reciprocal(out=tvz[:, :, 1], in_=tvz[:, :, 0]), [Btv], [Btv])
                    TT("dve", IJg4[:, 2, :, :], r0f, tvz[:, :, 1:2].to_broadcast([128, 8, 16]), ALU.mult, [Btv], [BIJ])
                return f

            def s_idxB(a):
                def f():
                    asl = slice(a * 128, (a + 1) * 128)
                    for w_ in range(3):
                        TR(PS[4][:, w_ * 128:(w_ + 1) * 128], IJg[:, w_, :], identf, [BIJ, Bc], [BP[4]])
                    CP("act", ITs[:, :, asl], PS[4][:, 0:384].rearrange("p (w t) -> p w t", w=3), [BP[4]], [BIT])
                return f

            def s_final():
                CP("act", JTb, ITs[:, 1, :], [BIT], [BIT])
                CP("act", ITb, ITs[:, 0, :], [BIT], [BIT])

            wqsteps = [s_wq(c) for c in range(16)]
            steps.append(s_load(0)); steps.append(wqsteps[0]); steps.append(wqsteps[1]); steps.append(s_loadB(0))
            steps.append(s_load(1)); steps.append(wqsteps[2]); steps.append(s_loadB(1))
            for c in range(16):
                if c + 3 < 16:
                    steps.append(wqsteps[c + 3])
                steps.append(s_q(c))
            for q4 in range(4):
                steps.append(s_scores(0, q4))
            for hp in range(16):
                steps.append(s_top(hp))
            steps.append(s_cand)
            for h in range(8):
                steps.append(s_top2(h))
            steps.append(s_idx(0))
            for q4 in range(4):
                steps.append(s_scores(1, q4))
            steps.append(s_idxB(0))
            for hp in range(16):
                steps.append(s_top(hp))
            steps.append(s_cand)
            for h in range(8):
                steps.append(s_top2(h))
            steps.append(s_idx(1))
            steps.append(s_idxB(1))
            steps.append(s_final)
            return steps

        def cons_steps(T, half):
            steps = []
            G_ = GTh[half]

            def prod(hb_):
                def f():
                    k = hb_ % 2
                    ts = slice(hb_ * HT, (hb_ + 1) * HT)
                    TT("dve", BfT[k], iota.unsqueeze(1).to_broadcast([128, HT, 128]), JTb[:, ts].unsqueeze(2).to_broadcast([128, HT, 128]),
                       ALU.is_equal, [BIT, Bc], [BBf[k]])
                    TT("dve", AfT[k], iota[:, half * 64:(half + 1) * 64].unsqueeze(1).to_broadcast([128, HT, 64]),
                       ITb[:, ts].unsqueeze(2).to_broadcast([128, HT, 64]), ALU.is_equal, [BIT, Bc], [BAf[k]])
                    TT("dve", AfT[k], AfT[k], ITs[:, 2, ts].unsqueeze(2).to_broadcast([128, HT, 64]), ALU.mult, [BAf[k], BIT], [BAf[k]])
                return f

            def mm(hb_):
                def f():
                    k = hb_ % 2
                    for q8 in range(HT // 8):
                        b_ = 4 + q8 % 2
                        for j in range(8):
                            tl = q8 * 8 + j
                            MM(PS[b_][:, j * 64:(j + 1) * 64], BfT[k][:, tl, :], AfT[k][:, tl, :], True, True, [BAf[k], BBf[k]], [BP[b_]])
                        tg = hb_ * HT + q8 * 8
                        CP("act", G_[:, tg:tg + 8, :], PS[b_].rearrange("p (t i) -> p t i", t=8), [BP[b_]], [BGT[half]])
                return f
            nb = PT // HT
            steps.append(prod(0))
            for hb_ in range(nb):
                if hb_ + 1 < nb:
                    p_, m_ = prod(hb_ + 1), mm(hb_)
                    steps.append(lambda p_=p_, m_=m_: (p_(), m_()))
                else:
                    steps.append(mm(hb_))
            return steps

        def uv_load(g):
            if g >= NTILE * 128:
                return
            us = g % NUV
            i = g % 128
            DMA(uvt[us], uv_c[i], [Bw["uv_r"]], [Buv[us]])

        PF = NUV - 4
        NSLOT = 4

        def U_(g):
            T, i = divmod(g, 128)
            pb_ = T % 2
            uv_load(g + PF)
            us = g % NUV
            uT = uv[us][0].rearrange("p (a b) -> p a b", a=8)
            k = g % NSLOT
            pa = PS[6 + g % 2][:, 0:PT]
            bpa = BPA[g % 2]
            for kc in range(8):
                MM(pa, uT[:, kc, :], h2T[pb_][:, kc, :], kc == 0, kc == 7, [Buv[us], Bh2T[pb_]], [bpa])
            ACT(glb[k], pa, AF.Gelu, [bpa], [Bgl[k]])
            TT("pool", wTb[k], glb[k], GTh[i // 64][:, :, i % 64], ALU.mult, [Bgl[k], BGT[i // 64]], [Bgl[k]])

        def V_(g):
            T, i = divmod(g, 128)
            us = g % NUV
            k = g % NSLOT
            for a in range(NST):
                for hf in range(2):
                    MM(PS[a * 2 + hf], wTb[k][:, a * 128:(a + 1) * 128], uv[us][1][:, hf * 512:(hf + 1) * 512],
                       i == 0, i == 127, [Bgl[k], Buv[us]], [BP[a * 2 + hf]])

        def finish(T):
            pb_ = T % 2
            t0 = T * PT
            for a in range(NST):
                for hf in range(2):
                    hs = slice(hf * 512, (hf + 1) * 512)
                    TT("dve", x2k[pb_][:, a, hs], x2k[pb_][:, a, hs], PS[a * 2 + hf], ALU.add, [Bx2k[pb_], BP[a * 2 + hf]], [Bx2k[pb_]])
                PDMA(out[t0 + a * 128:t0 + (a + 1) * 128, :], x2k[pb_][:, a, :], [Bx2k[pb_]], [Bout], sembuf=Bx2k[pb_])

        for f in topk_steps(0) + cons_steps(0, 0):
            f()
        for g in range(PF):
            uv_load(g)
        NG = NTILE * 128
        LA = 3
        for g in range(LA):
            U_(g)
        for T in range(NTILE):
            c1 = cons_steps(T, 1)
            tk = topk_steps(T + 1) if T + 1 < NTILE else []
            c0 = cons_steps(T + 1, 0) if T + 1 < NTILE else []
            split = len(tk) - 29 if tk else 0
            st_a = c1 + tk[:split]
            st_b = tk[split:] + c0
            wt_a = [2] * len(c1) + [1] * len(tk[:split])
            wt_b = [1] * len(tk[split:]) + [2] * len(c0)
            for (lo, steps, wts) in ((0, st_a, wt_a), (64, st_b, wt_b)):
                ns = len(steps)
                done = 0
                wtot = sum(wts)
                wdone = 0
                for j in range(64):
                    g = T * 128 + lo + j
                    if g + LA < NG:
                        U_(g + LA)
                    V_(g)
                    wtgt = (wtot * (j + 1)) // 56
                    while done < ns and wdone + wts[done] <= wtgt:
                        steps[done]()
                        wdone += wts[done]
                        done += 1
                    if j == 63:
                        while done < ns:
                            steps[done]()
                            done += 1
            finish(T)
        S.barrier()
        S.replay()
    return nc


def _consts():
    c = {}
    c["identb"] = np.eye(128).astype(NPBF)
    c["identf"] = np.eye(128, dtype=np.float32)
    kk = np.arange(128)
    c["tri"] = (kk[None, :] >= kk[:, None]).astype(NPBF)
    c["blk64"] = ((kk[:, None] // 64) == (kk[None, :] // 64)).astype(NPBF)
    c["iota128"] = np.tile(np.arange(128, dtype=np.float32), (128, 1)).astype(NPBF)
    c["iota16"] = np.tile(np.arange(16, dtype=np.float32), (128, 1))
    pos = np.arange(SEQ)
    qq = (pos % 128).astype(np.float32); qb = (pos // 128).astype(np.float32)
    qpos = np.zeros((8, 4, SEQ), np.float32); kpos = np.zeros((8, 4, SEQ), np.float32)
    for h in range(8):
        sl = 2.0 ** (-(h + 1))
        qpos[h, 0] = -sl * qq; qpos[h, 1] = -sl * 128.0 * qb; qpos[h, 2] = 1.0; qpos[h, 3] = 1.0
        kpos[h, 0] = 1.0; kpos[h, 1] = 1.0; kpos[h, 2] = sl * qq; kpos[h, 3] = sl * 128.0 * qb
    c["qpos"] = qpos.astype(NPBF); c["kpos"] = kpos.astype(NPBF)
    return c


def _layouts(w_in, w_attn_proj, w_conv_proj, w_out, peer_w_query, peer_sub_keys, peer_u, peer_v):
    f = lambda a: np.ascontiguousarray(a, dtype=np.float32)
    L = {}
    colchunk = lambda w, ncc: f(w.reshape(8, 128, ncc, 128).transpose(2, 1, 0, 3)).reshape(ncc * 128, 1024)
    rowmajor = lambda w: f(w.reshape(8, 128, w.shape[1]).transpose(1, 0, 2)).reshape(-1, 1024)
    L["w_in_r"] = colchunk(w_in, 64)
    L["wv_r"] = rowmajor(w_in[:, 2048:3072])
    L["wap_r"] = colchunk(w_attn_proj, 8)
    L["wcp_r"] = colchunk(w_conv_proj, 8)
    L["wout_r"] = rowmajor(w_out)
    L["wq_r"] = colchunk(peer_w_query, 16)
    L["sk_r"] = f(peer_sub_keys.reshape(16, 128, 128).transpose(2, 0, 1)).reshape(256, 1024)
    u_r = peer_u.reshape(128, 128, 8, 128).transpose(0, 3, 2, 1).reshape(128, 128, 1, 1024)
    v_r = peer_v.reshape(128, 128, 1, 1024)
    L["uv_r"] = f(np.concatenate([u_r, v_r], axis=2)).reshape(32768, 1024)
    return L


_NC_CACHE = {}


def kernel(x, norm_mix_g, w_in, q_norm_g, k_norm_g, lambda_q1, lambda_k1, lambda_q2, lambda_k2,
           subln_g, w_attn_proj, conv_w, w_conv_proj, w_out, norm_ffn_g,
           peer_w_query, peer_sub_keys, peer_u, peer_v, _dbg=None):
    A_ = lambda a: np.asarray(a)
    x = A_(x).astype(np.float32, copy=False)
    L = _layouts(A_(w_in)[0], A_(w_attn_proj)[0], A_(w_conv_proj)[0], A_(w_out)[0], A_(peer_w_query)[0],
                 A_(peer_sub_keys)[0], A_(peer_u)[0], A_(peer_v)[0])
    C = _consts()
    shared = dict(L)
    shared.update(C)
    shared["gmix"] = np.ascontiguousarray(A_(norm_mix_g)[0], np.float32)
    shared["gffn"] = np.ascontiguousarray(A_(norm_ffn_g)[0], np.float32)
    shared["qng"] = np.ascontiguousarray(A_(q_norm_g)[0], np.float32)
    shared["kng"] = np.ascontiguousarray(A_(k_norm_g)[0], np.float32)
    shared["lams"] = np.ascontiguousarray(np.stack([A_(lambda_q1)[0], A_(lambda_k1)[0], A_(lambda_q2)[0], A_(lambda_k2)[0]]), np.float32)
    shared["gsub"] = np.ascontiguousarray(A_(subln_g)[0], np.float32)
    shared["cw"] = np.ascontiguousarray(A_(conv_w)[0].reshape(3, 8, 128).transpose(2, 1, 0).reshape(128, 24), np.float32)
    key = repr(sorted(_dbg.items())) if _dbg else ""
    if key not in _NC_CACHE:
        _NC_CACHE[key] = build_program(_dbg)
    nc = _NC_CACHE[key]
    xs = x.reshape(NCORES, NTOK, D)
    in_maps = []
    for c in range(NCORES):
        m = dict(shared)
        m["x"] = np.ascontiguousarray(xs[c])
        in_maps.append(m)
    res = run_bass_kernel_spmd(nc, in_maps, core_ids=list(range(NCORES)))
    outs = np.stack([np.asarray(r["out"]) for r in res.results]).reshape(16, SEQ, D).astype(np.float32)
    if _dbg:
        return outs, res.results
    return outs
```
